# Optimizing a Trainium2 kernel written in Bass

```python
import math
import jax, jax.numpy as jnp
from jax import lax
import numpy as np

D_MODEL = 1024
BATCH = 8
SEQ = 2048
DEPTH = 1
DEC_BATCH = 16
DEC_SEQ = 2048
PAST_LEN = 128

GRID_W = 64
NA_HEADS = 16
NA_HEAD_DIM = 64
NA_WIDTH = NA_HEADS * NA_HEAD_DIM
WIN_R = 8
WIN_C = 16
QCB = WIN_C
KCB = 2 * WIN_C
NCB = GRID_W // QCB
SSD_EXPAND = 2
D_INNER = SSD_EXPAND * D_MODEL
SSD_HEAD_DIM = 64
SSD_HEADS = D_INNER // SSD_HEAD_DIM
SSD_GROUPS = 8
SSD_STATE = 128
SSD_CONV = 5
SSD_CHUNK = 128
CONV_CH = D_INNER + 2 * SSD_GROUPS * SSD_STATE
MEM_TOKENS = 256
MEM_HEADS = 4
MEM_HEAD_DIM = 256
MEM_WIDTH = MEM_HEADS * MEM_HEAD_DIM
N_BRANCH = 3
D_FF = 2816
IN_SIZES = (NA_WIDTH, NA_WIDTH, NA_WIDTH, MEM_WIDTH, D_INNER, CONV_CH, 2 * SSD_HEADS, N_BRANCH * D_MODEL)
IN_WIDTH = 3 * NA_WIDTH + MEM_WIDTH + D_INNER + CONV_CH + 2 * SSD_HEADS + N_BRANCH * D_MODEL
RMS_EPS = 1e-6
NEG_INF = -1e30

kernel_name = 'hybrid_na_ssd_memory_encoder'


def split_sizes(x, sizes):
    idx = [int(i) for i in np.cumsum(sizes)[:-1]]
    return jnp.split(x, idx, axis=-1)


def rms_norm(x, g):
    xf = x.astype(jnp.float32)
    y = xf * lax.rsqrt(jnp.mean(xf * xf, axis=-1, keepdims=True) + RMS_EPS)
    return (y * g.astype(jnp.float32)).astype(x.dtype)


def grouped_rms_norm(x, g, groups):
    shp = x.shape
    xg = x.reshape(shp[:-1] + (groups, shp[-1] // groups))
    return rms_norm(xg, g.reshape(groups, shp[-1] // groups)).reshape(shp)


def swiglu(x, w_gate, w_up, w_down):
    return (jax.nn.silu(x @ w_gate) * (x @ w_up)) @ w_down


def depthwise_conv(x, w, b):
    y = lax.conv_general_dilated(x, w[:, None, :], window_strides=(1,),
                                 padding=[(SSD_CONV // 2, SSD_CONV // 2)],
                                 dimension_numbers=('NWC', 'WIO', 'NWC'),
                                 feature_group_count=x.shape[-1])
    return y + b


def neighborhood_attention(q, k, v, rpb):
    bsz, t, h, dh = q.shape
    rows = t // GRID_W
    kr = min(WIN_R, rows)
    q = q.reshape(bsz, rows, GRID_W, h, dh)
    k = k.reshape(bsz, rows, GRID_W, h, dh)
    v = v.reshape(bsz, rows, GRID_W, h, dh)
    key_c0 = np.clip(np.arange(NCB) * QCB - WIN_C // 2, 0, GRID_W - KCB)
    kcol = key_c0[:, None] + np.arange(KCB)[None, :]
    qcol = np.arange(NCB)[:, None] * QCB + np.arange(QCB)[None, :]
    col_start = np.clip(qcol - WIN_C // 2, 0, GRID_W - WIN_C)
    kc = kcol[:, None, :]
    col_valid = (kc >= col_start[..., None]) & (kc < col_start[..., None] + WIN_C)
    rel_col = np.clip(kc - qcol[..., None] + WIN_C - 1, 0, 2 * WIN_C - 2)
    rpb_col = rpb.astype(jnp.float32)[:, :, rel_col]
    valid = jnp.asarray(col_valid)[None, None, :, :, None, :]
    scale = dh ** -0.5

    def row_block(r):
        rs = jnp.clip(r - kr // 2, 0, rows - kr)
        q_r = lax.dynamic_index_in_dim(q, r, axis=1, keepdims=False).reshape(bsz, NCB, QCB, h, dh)
        k_b = lax.dynamic_slice_in_dim(k, rs, kr, axis=1)[:, :, kcol]
        v_b = lax.dynamic_slice_in_dim(v, rs, kr, axis=1)[:, :, kcol]
        s = jnp.einsum('bnqhd,brnkhd->bhnqrk', q_r, k_b, preferred_element_type=jnp.float32) * scale
        rel_row = rs + jnp.arange(kr) - r + (WIN_R - 1)
        bias = jnp.transpose(rpb_col[:, rel_row], (0, 2, 3, 1, 4))
        s = jnp.where(valid, s + bias[None], NEG_INF)
        p = jax.nn.softmax(s.reshape(s.shape[:4] + (kr * KCB,)), axis=-1).reshape(s.shape)
        o = jnp.einsum('bhnqrk,brnkhd->bnqhd', p.astype(v.dtype), v_b)
        return o.reshape(bsz, GRID_W, h, dh)

    out = lax.map(row_block, jnp.arange(rows))
    return jnp.transpose(out, (1, 0, 2, 3, 4)).reshape(bsz, t, h * dh)


def ssd_chunked(x, dt, a, bm, cm):
    x = x.astype(jnp.float32)
    dt = dt.astype(jnp.float32)
    bm = bm.astype(jnp.float32)
    cm = cm.astype(jnp.float32)
    bsz, t, h, p = x.shape
    g, n = bm.shape[2], bm.shape[3]
    r = h // g
    l = SSD_CHUNK
    nc = t // l
    xc = (x * dt[..., None]).reshape(bsz, nc, l, g, r, p)
    bc = bm.reshape(bsz, nc, l, g, n)
    cc = cm.reshape(bsz, nc, l, g, n)
    a_cum = jnp.cumsum((dt * a).reshape(bsz, nc, l, g, r), axis=2)
    seg = a_cum[:, :, :, None] - a_cum[:, :, None, :]
    lower = jnp.asarray(np.tril(np.ones((l, l), dtype=bool)))[:, :, None, None]
    decay = jnp.exp(jnp.where(lower, seg, -jnp.inf))
    cb = jnp.einsum('bclgn,bcsgn->bclsg', cc, bc)
    y_diag = jnp.einsum('bclsgr,bcsgrp->bclgrp', cb[..., None] * decay, xc)
    decay_s = jnp.exp(a_cum[:, :, -1:] - a_cum)
    states = jnp.einsum('bclgn,bclgrp->bcgrpn', bc, xc * decay_s[..., None])
    chunk_decay = jnp.exp(a_cum[:, :, -1])

    def step(hs, inp):
        s, d = inp
        return hs * d[..., None, None] + s, hs

    h0 = jnp.zeros((bsz, g, r, p, n), jnp.float32)
    _, prev = lax.scan(step, h0, (jnp.moveaxis(states, 1, 0), jnp.moveaxis(chunk_decay, 1, 0)))
    prev = jnp.moveaxis(prev, 0, 1)
    y_off = jnp.einsum('bclgn,bcgrpn->bclgrp', cc, prev) * jnp.exp(a_cum)[..., None]
    return (y_diag + y_off).reshape(bsz, t, h, p)


def memory_attention(q, k, v):
    bsz, t, h, dh = q.shape
    s = jnp.einsum('bthd,bmhd->bhtm', q, k, preferred_element_type=jnp.float32) * dh ** -0.5
    pr = jax.nn.softmax(s, axis=-1).astype(v.dtype)
    return jnp.einsum('bhtm,bmhd->bthd', pr, v).reshape(bsz, t, h * dh)


def encoder_layer(x, mem, ffn1_norm, ffn1_w_gate, ffn1_w_up, ffn1_w_down, mix_norm, w_in,
                  na_q_norm, na_k_norm, na_rpb, conv_w, conv_b, dt_bias_f, dt_bias_b,
                  a_log_f, a_log_b, ssd_d, ssd_norm, mem_norm, w_mem_kv, mem_q_norm, mem_k_norm,
                  w_br_na, w_br_ssd, w_br_mem, w_out, ffn2_norm, ffn2_w_gate, ffn2_w_up, ffn2_w_down):
    bsz, t, _ = x.shape
    x = x + 0.5 * swiglu(rms_norm(x, ffn1_norm), ffn1_w_gate, ffn1_w_up, ffn1_w_down)
    u = rms_norm(x, mix_norm)
    proj = u @ w_in
    q_na, k_na, v_na, q_mem, z, xbc, dt_raw, gate_logits = split_sizes(proj, IN_SIZES)
    q_na = rms_norm(q_na.reshape(bsz, t, NA_HEADS, NA_HEAD_DIM), na_q_norm)
    k_na = rms_norm(k_na.reshape(bsz, t, NA_HEADS, NA_HEAD_DIM), na_k_norm)
    v_na = v_na.reshape(bsz, t, NA_HEADS, NA_HEAD_DIM)
    o_na = neighborhood_attention(q_na, k_na, v_na, na_rpb)
    xbc = jax.nn.silu(depthwise_conv(xbc, conv_w, conv_b))
    xs, bm, cm = split_sizes(xbc, (D_INNER, SSD_GROUPS * SSD_STATE, SSD_GROUPS * SSD_STATE))
    xs = xs.reshape(bsz, t, SSD_HEADS, SSD_HEAD_DIM)
    bm = bm.reshape(bsz, t, SSD_GROUPS, SSD_STATE)
    cm = cm.reshape(bsz, t, SSD_GROUPS, SSD_STATE)
    dt_f, dt_b = split_sizes(dt_raw.astype(jnp.float32), (SSD_HEADS, SSD_HEADS))
    dt_f = jax.nn.softplus(dt_f + dt_bias_f.astype(jnp.float32))
    dt_b = jax.nn.softplus(dt_b + dt_bias_b.astype(jnp.float32))
    a_f = -jnp.exp(a_log_f.astype(jnp.float32))
    a_b = -jnp.exp(a_log_b.astype(jnp.float32))
    y_fwd = ssd_chunked(xs, dt_f, a_f, bm, cm)
    y_bwd = jnp.flip(ssd_chunked(jnp.flip(xs, 1), jnp.flip(dt_b, 1), a_b, jnp.flip(bm, 1), jnp.flip(cm, 1)), 1)
    y_ssd = y_fwd + y_bwd + ssd_d.astype(jnp.float32)[:, None] * xs.astype(jnp.float32)
    y_ssd = y_ssd.reshape(bsz, t, D_INNER).astype(x.dtype) * jax.nn.silu(z)
    o_ssd = grouped_rms_norm(y_ssd, ssd_norm, SSD_GROUPS)
    kv_m = rms_norm(mem, mem_norm) @ w_mem_kv
    k_m, v_m = split_sizes(kv_m, (MEM_WIDTH, MEM_WIDTH))
    m = mem.shape[1]
    q_m = rms_norm(q_mem.reshape(bsz, t, MEM_HEADS, MEM_HEAD_DIM), mem_q_norm)
    k_m = rms_norm(k_m.reshape(bsz, m, MEM_HEADS, MEM_HEAD_DIM), mem_k_norm)
    v_m = v_m.reshape(bsz, m, MEM_HEADS, MEM_HEAD_DIM)
    o_mem = memory_attention(q_m, k_m, v_m)
    g_na, g_ssd, g_mem = split_sizes(jax.nn.sigmoid(gate_logits), (D_MODEL, D_MODEL, D_MODEL))
    merged = g_na * (o_na @ w_br_na) + g_ssd * (o_ssd @ w_br_ssd) + g_mem * (o_mem @ w_br_mem)
    x = x + merged @ w_out
    x = x + 0.5 * swiglu(rms_norm(x, ffn2_norm), ffn2_w_gate, ffn2_w_up, ffn2_w_down)
    return x


def setup_inputs(seed: int = 0) -> dict:
    key = jax.random.key(seed)
    ks = iter(jax.random.split(key, 64))

    def nrm(shape, scale):
        return scale * jax.random.normal(next(ks), shape, jnp.float32)

    def gain(n):
        return 1.0 + nrm((DEPTH, n), 0.02)

    def dt_bias():
        dt = jnp.exp(jax.random.uniform(next(ks), (DEPTH, SSD_HEADS), jnp.float32,
                                        minval=math.log(1e-3), maxval=math.log(1e-1)))
        return dt + jnp.log(-jnp.expm1(-dt))

    def a_log():
        return jnp.log(jax.random.uniform(next(ks), (DEPTH, SSD_HEADS), jnp.float32, minval=1.0, maxval=16.0))

    return {
        'x_prompt': nrm((BATCH, SEQ, D_MODEL), 1.0),
        'x_sample': nrm((DEC_BATCH, DEC_SEQ, D_MODEL), 1.0),
        'mem_prompt': nrm((BATCH, MEM_TOKENS, D_MODEL), 1.0),
        'mem_sample': nrm((DEC_BATCH, MEM_TOKENS, D_MODEL), 1.0),
        'ffn1_norm': gain(D_MODEL),
        'ffn1_w_gate': nrm((DEPTH, D_MODEL, D_FF), D_MODEL ** -0.5),
        'ffn1_w_up': nrm((DEPTH, D_MODEL, D_FF), D_MODEL ** -0.5),
        'ffn1_w_down': nrm((DEPTH, D_FF, D_MODEL), D_FF ** -0.5),
        'mix_norm': gain(D_MODEL),
        'w_in': nrm((DEPTH, D_MODEL, IN_WIDTH), D_MODEL ** -0.5),
        'na_q_norm': gain(NA_HEAD_DIM),
        'na_k_norm': gain(NA_HEAD_DIM),
        'na_rpb': nrm((DEPTH, NA_HEADS, 2 * WIN_R - 1, 2 * WIN_C - 1), 0.02),
        'conv_w': nrm((DEPTH, SSD_CONV, CONV_CH), SSD_CONV ** -0.5),
        'conv_b': nrm((DEPTH, CONV_CH), 0.02),
        'dt_bias_f': dt_bias(),
        'dt_bias_b': dt_bias(),
        'a_log_f': a_log(),
        'a_log_b': a_log(),
        'ssd_d': 1.0 + nrm((DEPTH, SSD_HEADS), 0.02),
        'ssd_norm': gain(D_INNER),
        'mem_norm': gain(D_MODEL),
        'w_mem_kv': nrm((DEPTH, D_MODEL, 2 * MEM_WIDTH), D_MODEL ** -0.5),
        'mem_q_norm': gain(MEM_HEAD_DIM),
        'mem_k_norm': gain(MEM_HEAD_DIM),
        'w_br_na': nrm((DEPTH, NA_WIDTH, D_MODEL), NA_WIDTH ** -0.5),
        'w_br_ssd': nrm((DEPTH, D_INNER, D_MODEL), D_INNER ** -0.5),
        'w_br_mem': nrm((DEPTH, MEM_WIDTH, D_MODEL), MEM_WIDTH ** -0.5),
        'w_out': nrm((DEPTH, D_MODEL, D_MODEL), D_MODEL ** -0.5),
        'ffn2_norm': gain(D_MODEL),
        'ffn2_w_gate': nrm((DEPTH, D_MODEL, D_FF), D_MODEL ** -0.5),
        'ffn2_w_up': nrm((DEPTH, D_MODEL, D_FF), D_MODEL ** -0.5),
        'ffn2_w_down': nrm((DEPTH, D_FF, D_MODEL), D_FF ** -0.5),
    }


def reference(x_prompt, x_sample, mem_prompt, mem_sample, ffn1_norm, ffn1_w_gate, ffn1_w_up,
              ffn1_w_down, mix_norm, w_in, na_q_norm, na_k_norm, na_rpb, conv_w, conv_b,
              dt_bias_f, dt_bias_b, a_log_f, a_log_b, ssd_d, ssd_norm, mem_norm, w_mem_kv,
              mem_q_norm, mem_k_norm, w_br_na, w_br_ssd, w_br_mem, w_out, ffn2_norm,
              ffn2_w_gate, ffn2_w_up, ffn2_w_down):
    layer_weights = (ffn1_norm, ffn1_w_gate, ffn1_w_up, ffn1_w_down, mix_norm, w_in,
                     na_q_norm, na_k_norm, na_rpb, conv_w, conv_b, dt_bias_f, dt_bias_b,
                     a_log_f, a_log_b, ssd_d, ssd_norm, mem_norm, w_mem_kv, mem_q_norm,
                     mem_k_norm, w_br_na, w_br_ssd, w_br_mem, w_out, ffn2_norm, ffn2_w_gate,
                     ffn2_w_up, ffn2_w_down)
    y_prompt = x_prompt
    y_sample = x_sample
    for l in range(DEPTH):
        w_l = [w[l] for w in layer_weights]
        y_prompt = encoder_layer(y_prompt, mem_prompt, *w_l)
        y_sample = encoder_layer(y_sample, mem_sample, *w_l)
    return (y_prompt, y_sample)
```

```python
import numpy as np
import concourse.bass as bass
import concourse.mybir as mybir
from concourse.bass_utils import run_bass_kernel_spmd

F32 = mybir.dt.float32
BF16 = mybir.dt.bfloat16
U8 = mybir.dt.uint8
AF = mybir.ActivationFunctionType
ALU = mybir.AluOpType

D = 1024
T = 2048
NSEQ = 3
DFF = 2816
NHC = DFF // 128
INW = 13376
MEMT = 256
EPS = 1e-6
NEG = -30000.0
Q0, K0, V0, QM0, Z0, X0, B0, C0, DT0, G0 = 0, 1024, 2048, 3072, 4096, 6144, 8192, 9216, 10240, 10304

ENABLE = {"mem": True, "na": True, "ssd": True}


def _na_chunks(i):
    if i <= 1:
        return 0, 4
    if i >= 14:
        return 12, 4
    return i - 2, 5


def _na_blocks(j):
    bl = [i for i in range(16) if _na_chunks(i)[0] <= j < _na_chunks(i)[0] + _na_chunks(i)[1]]
    return bl[0], bl[-1]


_NA_SLOT = {}
_o = 0
for _j in (0, 1, 2, 3):
    _NA_SLOT[_j] = _o
    _o += _na_blocks(_j)[1] - _na_blocks(_j)[0] + 1
_NA_INT = _o
_o += 5
for _j in (12, 13, 14, 15):
    _NA_SLOT[_j] = _o
    _o += _na_blocks(_j)[1] - _na_blocks(_j)[0] + 1
for _j in range(4, 12):
    _NA_SLOT[_j] = _NA_INT
NA_NSLOT = _o


class Tok:
    __slots__ = ("sem", "val", "eng")

    def __init__(self, sem, val, eng):
        self.sem, self.val, self.eng = sem, val, eng


class Tl:
    __slots__ = ("ap", "w", "r", "psum")

    def __init__(self, ap=None, psum=False):
        self.ap = ap
        self.w = None
        self.r = {}
        self.psum = psum


class KB:
    NDS = 24

    def __init__(self, nc):
        self.nc = nc
        self.engs = {"pe": nc.tensor, "act": nc.scalar, "dve": nc.vector, "pool": nc.gpsimd, "sp": nc.sync}
        self.sem = {e: nc.alloc_semaphore("s_" + e) for e in ("pe", "act", "dve", "pool")}
        self.cnt = {e: 0 for e in self.sem}
        self.waited = {e: {} for e in self.engs}
        self.dsem = [nc.alloc_semaphore("d%d" % i) for i in range(self.NDS)]
        self.dcnt = [0] * self.NDS
        self.dlast = [None] * self.NDS
        self.di = {"sp": 0, "pool": 0}
        self.dslots = {"sp": list(range(0, 16)), "pool": list(range(16, self.NDS))}
        self.last = {}
        self.arena = nc.alloc_sbuf_tensor("arena", [128, 206 * 1024], U8)
        self.ps = nc.alloc_psum_tensor("ps", [128, 4096], F32)
        self.nops = 0

    def view(self, off, n, dt):
        sz = mybir.dt.size(dt)
        assert off % 4 == 0 and off + n * sz <= 206 * 1024, (off, n, sz)
        return self.arena[:, off:off + n * sz].bitcast(dt)

    def bank(self, b, nb=1):
        return self.ps[:, b * 512:(b + nb) * 512]

    def bank_bf(self, b):
        return self.ps[:, b * 512:(b + 1) * 512].bitcast(BF16)

    def wait(self, eng, tok):
        if tok is None:
            return
        if tok.eng == eng and eng in ("pe", "sp"):
            return
        key = id(tok.sem)
        w = self.waited[eng]
        if w.get(key, 0) >= tok.val:
            return
        self.engs[eng].wait_ge(tok.sem, tok.val)
        w[key] = tok.val

    def deps(self, eng, reads, writes):
        for t in reads:
            self.wait(eng, t.w)
            if t.psum:
                for r in t.r.values():
                    self.wait(eng, r)
        for t in writes:
            self.wait(eng, t.w)
            for r in t.r.values():
                self.wait(eng, r)

    def mark(self, tok, reads, writes):
        for t in reads:
            if t.psum:
                t.w = tok
                t.r = {}
                continue
            t.r[tok.eng if tok.eng != "dma" else id(tok.sem)] = tok
        for t in writes:
            t.w = tok
            t.r = {}

    def op(self, eng, fn, reads=(), writes=()):
        self.deps(eng, reads, writes)
        inst = fn(self.engs[eng])
        self.cnt[eng] += 1
        inst.then_inc(self.sem[eng], 1)
        tok = Tok(self.sem[eng], self.cnt[eng], eng)
        self.last[eng] = tok
        self.mark(tok, reads, writes)
        self.nops += 1
        return tok

    def mm(self, out_t, out_ap, pairs, reads, start=True, stop=True, transpose=False, ident=None):
        self.deps("pe", reads, [out_t])
        n = len(pairs)
        inst = None
        for i, (l, r) in enumerate(pairs):
            inst = self.nc.tensor.matmul(out_ap, l, r, start=(start and i == 0), stop=(stop and i == n - 1))
        self.nops += n
        return self.mm_done(inst, reads, [out_t])

    def mm_raw(self, out_ap, l, r, start, stop, skip=False):
        self.nops += 1
        if skip:
            return self.nc.tensor.matmul(out_ap, l, r, start=start, stop=stop, skip_group_check=True)
        return self.nc.tensor.matmul(out_ap, l, r, start=start, stop=stop)

    def tr_raw(self, out_ap, in_ap, ident):
        self.nops += 1
        return self.nc.tensor.transpose(out_ap, in_ap, ident)

    def mm_done(self, inst, reads, writes):
        self.cnt["pe"] += 1
        inst.then_inc(self.sem["pe"], 1)
        tok = Tok(self.sem["pe"], self.cnt["pe"], "pe")
        self.last["pe"] = tok
        self.mark(tok, reads, writes)
        return tok

    def dma(self, out_ap, in_ap, reads=(), writes=(), q="sp"):
        sl = self.dslots[q]
        i = sl[self.di[q] % len(sl)]
        self.di[q] += 1
        self.deps(q, reads, writes)
        self.wait(q, self.dlast[i])
        inst = self.engs[q].dma_start(out=out_ap, in_=in_ap)
        self.dcnt[i] += 16
        inst.then_inc(self.dsem[i], 16)
        tok = Tok(self.dsem[i], self.dcnt[i], "dma")
        self.dlast[i] = tok
        self.mark(tok, reads, writes)
        self.nops += 1
        return tok

    def barrier(self):
        toks = [t for t in self.last.values()] + [t for t in self.dlast if t is not None]
        for e in self.engs:
            for t in toks:
                self.wait(e, t)

    def finish(self):
        for t in self.dlast:
            self.wait("sp", t)
        for t in self.last.values():
            self.wait("sp", t)


class Rot:
    def __init__(self, tiles):
        self.tiles = tiles
        self.i = 0

    def next(self):
        t = self.tiles[self.i % len(self.tiles)]
        self.i += 1
        return t


def bc(ap, shape):
    return ap.to_broadcast(list(shape))


def build_program(debug=None):
    nc = bass.Bass("TRN2", target_bir_lowering=False)
    kb = KB(nc)
    P = 128

    def din(name, shape, dt=F32):
        return nc.dram_tensor(name, list(shape), dt, kind="ExternalInput").ap()

    x_in = din("x", [NSEQ, T, D])
    mem_in = din("mem", [NSEQ, MEMT, D])
    y_out = nc.dram_tensor("y", [NSEQ, T, D], F32, kind="ExternalOutput").ap()
    wnames = {
        "ffn1_w_gate": (D, DFF), "ffn1_w_up": (D, DFF), "ffn1_w_down": (DFF, D),
        "w_in": (D, INW), "w_mem_kv": (D, 2048), "w_br_na": (1024, D), "w_br_ssd": (2048, D),
        "w_br_mem": (1024, D), "w_out": (D, D),
        "ffn2_w_gate": (D, DFF), "ffn2_w_up": (D, DFF), "ffn2_w_down": (DFF, D),
    }
    wf = {n: din(n, s) for n, s in wnames.items()}
    wb = {n: nc.dram_tensor(n + "_bf", list(s), BF16).ap() for n, s in wnames.items()}
    wbt = {n: Tl() for n in wnames}
    gains_in = din("gains", [P, 64])
    ident_in = din("ident", [P, P])
    ssdc_in = din("ssdc", [P, 352])
    tri_in = din("tri", [P, 512])
    nab_in = din("nab", [16 * 128, NA_NSLOT * 128])
    nab_bf = nc.dram_tensor("nab_bf", [16 * 128, NA_NSLOT * 128], BF16).ap()
    x1s = nc.dram_tensor("x1s", [T, D], F32).ap()
    x1s_t = [Tl() for _ in range(4)]

    off = 0

    def alloc(n, dt):
        nonlocal off
        v = kb.view(off, n, dt)
        off += n * mybir.dt.size(dt)
        off = (off + 31) // 32 * 32
        return v

    gains = alloc(64, F32); gains_t = Tl()
    identb = alloc(128, BF16); identb_t = Tl()
    identf = alloc(128, F32); identf_t = Tl()
    UT_OFF = off
    uT = alloc(8 * T, BF16).rearrange("p (c t) -> p c t", c=8)
    uT_t = [Tl() for _ in range(4)]
    MG_OFF = off
    mg = alloc(8 * T, BF16).rearrange("p (c t) -> p c t", c=8)
    mg_t = [Tl() for _ in range(4)]
    PH0 = off

    G_FFN1, G_MIX, G_FFN2, G_MEMN, G_NAQ, G_NAK, G_MEMQ, G_MEMK, G_SSDN = 0, 8, 16, 24, 32, 33, 34, 36, 38

    kb.dma(gains, gains_in, writes=[gains_t])
    kb.dma(identf, ident_in, writes=[identf_t])
    kb.dma(identb, ident_in, writes=[identb_t], q="pool")
    wtoks = {}
    FAST = ("ffn1_w_gate", "ffn1_w_up", "ffn1_w_down")
    stg_rot = Rot([(kb.view(PH0 + k * 11264, 2816, F32), Tl()) for k in range(2)])
    stb_rot = Rot([(kb.view(PH0 + 22528 + k * 5632, 2816, BF16), Tl()) for k in range(2)])
    ci_ = 0
    for n in FAST:
        r, c = wnames[n]
        wtoks[n] = []
        rows_per = 2816 // c
        nblk = r // 128
        for b0 in range(0, nblk, rows_per):
            nb_ = min(rows_per, nblk - b0)
            (stg, stg_t) = stg_rot.next()
            (stb, stb_t) = stb_rot.next()
            src = wf[n][b0 * 128:(b0 + nb_) * 128, :].rearrange("(k p) n -> p k n", p=128)
            dst = wb[n][b0 * 128:(b0 + nb_) * 128, :].rearrange("(k p) n -> p k n", p=128)
            kb.dma(stg[:, 0:nb_ * c].rearrange("p (k n) -> p k n", k=nb_), src, writes=[stg_t])
            if ci_ % 2 == 0:
                kb.op("dve", lambda e: e.tensor_copy(out=stb[:, 0:nb_ * c], in_=stg[:, 0:nb_ * c]), reads=[stg_t], writes=[stb_t])
            else:
                kb.op("act", lambda e: e.activation(out=stb[:, 0:nb_ * c], in_=stg[:, 0:nb_ * c], func=AF.Copy), reads=[stg_t], writes=[stb_t])
            ci_ += 1
            wtoks[n].append(kb.dma(dst, stb[:, 0:nb_ * c].rearrange("p (k n) -> p k n", k=nb_), reads=[stb_t]))
    for n_ in FAST:
        for t_ in wtoks[n_]:
            kb.wait("pool", t_)
    for n, (r, c) in wnames.items():
        if n in FAST:
            continue
        nblk = r // 128
        step = 2 if c > 4096 else 8
        wtoks[n] = []
        for b0 in range(0, nblk, step):
            b1 = min(nblk, b0 + step)
            wtoks[n].append(kb.dma(wb[n][b0 * 128:b1 * 128, :], wf[n][b0 * 128:b1 * 128, :], writes=[], q="pool"))
    wtoks["nab"] = []
    for b0 in range(0, 16, 4):
        wtoks["nab"].append(kb.dma(nab_bf[b0 * 128:(b0 + 4) * 128, :], nab_in[b0 * 128:(b0 + 4) * 128, :], q="pool"))
    for t in (identb_t.w, gains_t.w, identf_t.w):
        kb.wait("sp", t)
    wready = set()

    def need(n):
        if n in wready:
            return
        wready.add(n)
        for t in wtoks[n]:
            kb.wait("sp", t)

    def rmsnorm_to_T(xt, xt_t, ntok_chunks, gcol, outT, outT_t, col0, scr):
        ss, ss_t, junk, junk_t, xn_rot, pst_rot = scr
        J = ntok_chunks
        kb.op("dve", lambda e: e.memset(ss[:, 0:J], 0.0), writes=[ss_t])
        for j in range(J):
            kb.op("act", lambda e: e.activation(out=junk, in_=xt[:, j, :], func=AF.Square, accum_out=ss[:, j:j + 1]),
                  reads=[xt_t], writes=[junk_t, ss_t])
        kb.op("act", lambda e: e.activation(out=ss[:, 0:J], in_=ss[:, 0:J], func=AF.Sqrt, bias=EPS, scale=1.0 / D),
              writes=[ss_t])
        kb.op("dve", lambda e: e.reciprocal(ss[:, 0:J], ss[:, 0:J]), writes=[ss_t])
        for j in range(J):
            xn, xn_t = xn_rot.next()
            kb.op("dve", lambda e: e.tensor_scalar(out=xn, in0=xt[:, j, :], scalar1=ss[:, j:j + 1], scalar2=None,
                                                    op0=ALU.mult), reads=[xt_t, ss_t], writes=[xn_t])
            pb, pb_t = pst_rot.next()
            kb.deps("pe", [xn_t, identb_t], [pb_t])
            inst = None
            for c in range(8):
                inst = kb.tr_raw(pb[:, c * 128:(c + 1) * 128], xn[:, c * 128:(c + 1) * 128], identb)
            kb.mm_done(inst, [xn_t, identb_t], [pb_t])
            g = gains[:, gcol:gcol + 8]
            kb.op("dve", lambda e: e.tensor_tensor(out=outT[:, :, col0 + j * 128: col0 + (j + 1) * 128],
                                                   in0=pb.rearrange("p (c t) -> p c t", c=8),
                                                   in1=bc(g.unsqueeze(2), [P, 8, 128]), op=ALU.mult),
                  reads=[pb_t, gains_t], writes=[outT_t])

    def norm_front(xt, xt_t, L, slot="XN"):
        ss, ss_t, junk, junk_t, _, _ = L["nscr"]
        XN, XN_t = L[slot]
        kb.op("dve", lambda e: e.memset(ss[:, 0:4], 0.0), writes=[ss_t])
        for j in range(4):
            kb.op("act", lambda e: e.activation(out=junk, in_=xt[:, j, :], func=AF.Square, accum_out=ss[:, j:j + 1]),
                  reads=[xt_t], writes=[junk_t, ss_t])
        kb.op("act", lambda e: e.activation(out=ss[:, 0:4], in_=ss[:, 0:4], func=AF.Sqrt, bias=EPS, scale=1.0 / D),
              writes=[ss_t])
        kb.op("dve", lambda e: e.reciprocal(ss[:, 0:4], ss[:, 0:4]), writes=[ss_t])
        for j in range(4):
            kb.op("dve", lambda e: e.tensor_scalar(out=XN[:, j, :], in0=xt[:, j, :], scalar1=ss[:, j:j + 1], scalar2=None,
                                                   op0=ALU.mult), reads=[xt_t, ss_t], writes=[XN_t])

    def norm_back(gcol, L, slot="XN", out=None, col0=0):
        _, _, _, _, _, pst_rot = L["nscr"]
        XN, XN_t = L[slot]
        uf, uf_t = L["uf"] if out is None else out
        g = gains[:, gcol:gcol + 8]
        for j in range(4):
            pb, pb_t = pst_rot.next()
            kb.deps("pe", [XN_t, identb_t], [pb_t])
            inst = None
            for c in range(8):
                inst = kb.tr_raw(pb[:, c * 128:(c + 1) * 128], XN[:, j, c * 128:(c + 1) * 128], identb)
            kb.mm_done(inst, [XN_t, identb_t], [pb_t])
            kb.op("dve", lambda e: e.tensor_tensor(out=uf[:, :, col0 + j * 128:col0 + (j + 1) * 128],
                                                   in0=pb.rearrange("p (c t) -> p c t", c=8),
                                                   in1=bc(g.unsqueeze(2), [P, 8, 128]), op=ALU.mult),
                  reads=[pb_t, gains_t], writes=[uf_t])

    def gate_up(wg, wu, L):
        uf, uf_t = L["uf"]
        h, h_t = L["h"]
        wrot = L["wrot"]
        HB = 2
        nblk = NHC // HB

        def load(bi):
            need(wg); need(wu)
            (wgb, wgb_t), (wub, wub_t) = wrot.next()
            src_g = wb[wg].rearrange("(c p) n -> p c n", p=128)[:, :, bi * HB * 128:(bi + 1) * HB * 128]
            src_u = wb[wu].rearrange("(c p) n -> p c n", p=128)[:, :, bi * HB * 128:(bi + 1) * HB * 128]
            kb.dma(wgb, src_g, writes=[wgb_t])
            kb.dma(wub, src_u, writes=[wub_t])
            return (wgb, wgb_t), (wub, wub_t)

        nxt = load(0)
        for bi in range(nblk):
            (wgb, wgb_t), (wub, wub_t) = nxt
            if bi + 1 < nblk:
                nxt = load(bi + 1)
            for hl in range(HB):
                hc = bi * HB + hl
                pg, pg_t = L["pgu"].next()
                pu, pu_t = L["pgu"].next()
                kb.mm(pg_t, pg, [(wgb[:, c, hl * 128:(hl + 1) * 128], uf[:, c, :]) for c in range(8)], [wgb_t, uf_t])
                kb.mm(pu_t, pu, [(wub[:, c, hl * 128:(hl + 1) * 128], uf[:, c, :]) for c in range(8)], [wub_t, uf_t])
                sg, sg_t = L["sg"].next()
                kb.op("act", lambda e: e.activation(out=sg, in_=pg, func=AF.Silu), reads=[pg_t], writes=[sg_t])
                kb.op("dve", lambda e: e.tensor_tensor(out=h[:, hc, :], in0=pu, in1=sg, op=ALU.mult),
                      reads=[pu_t, sg_t], writes=[h_t])

    def down(xt, xt_t, xo, xo_t, wd_sb, wd_t, L):
        h, h_t = L["h"]
        for j in range(4):
            for hf in range(2):
                pd, pd_t = L["pd"].next()
                kb.mm(pd_t, pd, [(h[:, hc, j * 128:(j + 1) * 128], wd_sb[:, hc, hf * 512:(hf + 1) * 512])
                                 for hc in range(NHC)], [h_t, wd_t])
                if xo_t is xt_t:
                    kb.op("dve", lambda e: e.scalar_tensor_tensor(out=xo[:, j, hf * 512:(hf + 1) * 512], in0=pd, scalar=0.5,
                                                                  in1=xt[:, j, hf * 512:(hf + 1) * 512],
                                                                  op0=ALU.mult, op1=ALU.add),
                          reads=[pd_t], writes=[xo_t])
                else:
                    kb.op("dve", lambda e: e.scalar_tensor_tensor(out=xo[:, j, hf * 512:(hf + 1) * 512], in0=pd, scalar=0.5,
                                                                  in1=xt[:, j, hf * 512:(hf + 1) * 512],
                                                                  op0=ALU.mult, op1=ALU.add),
                          reads=[pd_t, xt_t], writes=[xo_t])

    def ffn_layout(base, hole, with_xo=True):
        nonlocal off
        L = {}
        off = hole
        L["xt"] = Rot([(alloc(4 * D, F32).rearrange("p (j d) -> p j d", j=4), Tl()) for _ in range(2)])
        off = base
        L["wd"] = (alloc(NHC * D, BF16).rearrange("p (h n) -> p h n", h=NHC), Tl())
        L["wrot"] = Rot([((alloc(8 * 256, BF16).rearrange("p (c n) -> p c n", c=8), Tl()),
                          (alloc(8 * 256, BF16).rearrange("p (c n) -> p c n", c=8), Tl())) for _ in range(2)])
        if with_xo:
            L["xo"] = Rot([(alloc(4 * D, F32).rearrange("p (j d) -> p j d", j=4), Tl()) for _ in range(1)])
        L["uf"] = (alloc(8 * 512, BF16).rearrange("p (c t) -> p c t", c=8), Tl())
        L["XN"] = (alloc(4 * D, BF16).rearrange("p (j d) -> p j d", j=4), Tl())
        if with_xo:
            L["XN2"] = (alloc(4 * D, BF16).rearrange("p (j d) -> p j d", j=4), Tl())
        L["h"] = (alloc(NHC * 512, BF16).rearrange("p (h t) -> p h t", h=NHC), Tl())
        L["sg"] = Rot([(alloc(512, F32), Tl()) for _ in range(2)])
        ss = (alloc(8, F32), Tl())
        junk = (alloc(D, F32), Tl())
        xn_rot = Rot([(alloc(D, BF16), Tl()) for _ in range(2)])
        pst_rot = Rot([(kb.bank_bf(0), Tl(psum=True)), (kb.bank_bf(1), Tl(psum=True))])
        L["nscr"] = (ss[0], ss[1], junk[0], junk[1], xn_rot, pst_rot)
        L["pgu"] = Rot([(kb.bank(2), Tl(psum=True)), (kb.bank(3), Tl(psum=True)), (kb.bank(4), Tl(psum=True)), (kb.bank(5), Tl(psum=True))])
        L["pd"] = Rot([(kb.bank(6), Tl(psum=True)), (kb.bank(7), Tl(psum=True))])
        return L

    def tok_major(dram2d, t):
        return dram2d[t * 512:(t + 1) * 512, :].rearrange("(j p) d -> p j d", p=128)

    def phase_A(s):
        L = ffn_layout(PH0, MG_OFF)
        wd_sb, wd_t = L["wd"]
        need("ffn1_w_down")
        kb.dma(wd_sb, wb["ffn1_w_down"].rearrange("(h p) n -> p h n", p=128), writes=[wd_t])
        tiles = []
        for t in range(2):
            xt, xt_t = L["xt"].next()
            kb.dma(xt, tok_major(x_in[s], t), writes=[xt_t])
            tiles.append((xt, xt_t))
        norm_front(tiles[0][0], tiles[0][1], L)
        for t in range(4):
            xt, xt_t = tiles[t]
            norm_back(G_FFN1, L)
            gate_up("ffn1_w_gate", "ffn1_w_up", L)
            if t >= 1:
                norm_back(G_MIX, L, slot="XN2", out=(uT, uT_t[t - 1]), col0=(t - 1) * 512)
            if t + 1 < 4:
                norm_front(tiles[t + 1][0], tiles[t + 1][1], L)
            xo, xo_t = L["xo"].next()
            down(xt, xt_t, xo, xo_t, wd_sb, wd_t, L)
            kb.dma(tok_major(x1s, t), xo, reads=[xo_t], writes=[x1s_t[t]])
            if t + 2 < 4:
                xn_, xn_t_ = L["xt"].next()
                kb.dma(xn_, tok_major(x_in[s], t + 2), writes=[xn_t_])
                tiles.append((xn_, xn_t_))
            norm_front(xo, xo_t, L, slot="XN2")
        norm_back(G_MIX, L, slot="XN2", out=(uT, uT_t[3]), col0=3 * 512)

    def phase_C(s):
        L = ffn_layout(PH0, UT_OFF, with_xo=False)
        wd_sb, wd_t = L["wd"]
        need("ffn2_w_down"); need("w_out")
        kb.dma(wd_sb, wb["ffn2_w_down"].rearrange("(h p) n -> p h n", p=128), writes=[wd_t])
        wo = alloc(8 * D, BF16).rearrange("p (c n) -> p c n", c=8)
        wo_t = Tl()
        kb.dma(wo, wb["w_out"].rearrange("(c p) n -> p c n", p=128), writes=[wo_t])

        def front(t, xt, xt_t):
            for j in range(4):
                for hf in range(2):
                    pd, pd_t = L["pd"].next()
                    kb.mm(pd_t, pd, [(mg[:, c, t * 512 + j * 128: t * 512 + (j + 1) * 128], wo[:, c, hf * 512:(hf + 1) * 512])
                                     for c in range(8)], [mg_t[t], wo_t])
                    kb.op("dve", lambda e: e.tensor_tensor(out=xt[:, j, hf * 512:(hf + 1) * 512], in0=pd,
                                                           in1=xt[:, j, hf * 512:(hf + 1) * 512], op=ALU.add),
                          reads=[pd_t], writes=[xt_t])
            norm_front(xt, xt_t, L)

        tiles = []
        for t in range(2):
            xt, xt_t = L["xt"].next()
            kb.dma(xt, tok_major(x1s, t), reads=[x1s_t[t]], writes=[xt_t])
            tiles.append((xt, xt_t))
        front(0, tiles[0][0], tiles[0][1])
        for t in range(4):
            xt, xt_t = tiles[t]
            norm_back(G_FFN2, L)
            gate_up("ffn2_w_gate", "ffn2_w_up", L)
            if t + 1 < 4:
                front(t + 1, tiles[t + 1][0], tiles[t + 1][1])
            down(xt, xt_t, xt, xt_t, wd_sb, wd_t, L)
            kb.dma(tok_major(y_out[s], t), xt, reads=[xt_t])
            if t + 2 < 4:
                xn_, xn_t_ = L["xt"].next()
                kb.dma(xn_, tok_major(x1s, t + 2), reads=[x1s_t[t + 2]], writes=[xn_t_])
                tiles.append((xn_, xn_t_))

    BR = debug.get("branches", ("na", "ssd", "mem")) if debug else ("na", "ssd", "mem")
    GCOL = {"na": G0, "ssd": G0 + 1024, "mem": G0 + 2048}
    win_v = wb["w_in"].rearrange("(c p) n -> p c n", p=128)

    gsc = nc.dram_tensor("gsc", [8, 128, T], F32).ap()
    gsc_t = [[Tl() for _ in range(4)] for _ in range(8)]

    def merge_branch(name, wname, krow0, oT, oT_t, first, L, nk=8, gate="compute"):
        need(wname); need("w_in")
        wv = wb[wname].rearrange("(k p) n -> p k n", p=128)
        k0 = krow0 // 128
        def wload(f):
            (wbr, wbr_t), (wgt, wgt_t) = L["mw"].next()
            kb.dma(wbr[:, 0:nk, :], wv[:, k0:k0 + nk, f * 128:(f + 1) * 128], writes=[wbr_t])
            if gate != "load":
                kb.dma(wgt, win_v[:, :, GCOL[name] + f * 128: GCOL[name] + (f + 1) * 128], writes=[wgt_t])
            return (wbr, wbr_t), (wgt, wgt_t)

        nxt_w = wload(0)
        for f in range(8):
            (wbr, wbr_t), (wgt, wgt_t) = nxt_w
            if f + 1 < 8:
                nxt_w = wload(f + 1)
            for t in range(4):
                pbr, pbr_t = L["mp"].next()
                sg, sg_t = L["msg"].next()
                if gate == "load":
                    kb.dma(sg, gsc[f, :, t * 512:(t + 1) * 512], reads=[gsc_t[f][t]], writes=[sg_t])
                else:
                    pgl, pgl_t = L["mp"].next()
                    kb.mm(pgl_t, pgl, [(wgt[:, c, :], uT[:, c, t * 512:(t + 1) * 512]) for c in range(8)], [wgt_t, uT_t[t]])
                kb.mm(pbr_t, pbr, [(wbr[:, k, :], oT[:, k, t * 512:(t + 1) * 512]) for k in range(nk)], [wbr_t, oT_t])
                if gate != "load":
                    kb.op("act", lambda e: e.activation(out=sg, in_=pgl, func=AF.Sigmoid), reads=[pgl_t], writes=[sg_t])
                if gate == "store":
                    kb.dma(gsc[f, :, t * 512:(t + 1) * 512], sg, reads=[sg_t], writes=[gsc_t[f][t]])
                if first:
                    kb.op("dve", lambda e: e.tensor_tensor(out=mg[:, f, t * 512:(t + 1) * 512], in0=pbr, in1=sg, op=ALU.mult),
                          reads=[pbr_t, sg_t], writes=[mg_t[t]])
                else:
                    kb.op("dve", lambda e: e.tensor_tensor(out=sg, in0=pbr, in1=sg, op=ALU.mult),
                          reads=[pbr_t], writes=[sg_t])
                    kb.op("dve", lambda e: e.tensor_tensor(out=mg[:, f, t * 512:(t + 1) * 512],
                                                            in0=mg[:, f, t * 512:(t + 1) * 512], in1=sg, op=ALU.add),
                          reads=[sg_t], writes=[mg_t[t]])

    def merge_layout():
        L = {}
        L["mw"] = Rot([((alloc(8 * 128, BF16).rearrange("p (c n) -> p c n", c=8), Tl()),
                        (alloc(8 * 128, BF16).rearrange("p (c n) -> p c n", c=8), Tl())) for _ in range(2)])
        L["msg"] = Rot([(alloc(512, F32), Tl()) for _ in range(2)])
        return L

    def mem_branch(s, first):
        nonlocal off
        off = PH0
        need("w_in"); need("w_mem_kv")
        oT = alloc(8 * T, BF16).rearrange("p (c t) -> p c t", c=8)
        oT_t = Tl()
        L = merge_layout()
        onesb = alloc(128, BF16); onesb_t = Tl()
        kb.op("pool", lambda e: e.memset(onesb, 1.0), writes=[onesb_t])
        memx = alloc(2 * D, F32).rearrange("p (j d) -> p j d", j=2); memx_t = Tl()
        mT = alloc(8 * 256, BF16).rearrange("p (c t) -> p c t", c=8); mT_t = Tl()
        ss = (alloc(8, F32), Tl()); junk = (alloc(D, F32), Tl())
        xn_rot = Rot([(alloc(D, BF16), Tl()) for _ in range(2)])
        bk = [(kb.bank(i), Tl(psum=True)) for i in range(8)]
        pst_rot = Rot([(kb.bank_bf(0), bk[0][1])])
        kTn = alloc(8 * 256, BF16).rearrange("p (c t) -> p c t", c=8); kTn_t = Tl()
        vm = alloc(2 * D, BF16).rearrange("p (j d) -> p j d", j=2); vm_t = Tl()
        wkv_rot = Rot([(alloc(8 * 512, BF16).rearrange("p (c n) -> p c n", c=8), Tl()) for _ in range(2)])
        wq = alloc(8 * D, BF16).rearrange("p (c n) -> p c n", c=8); wq_t = Tl()
        sq_rot = Rot([(alloc(512, BF16), Tl()) for _ in range(2)])
        rs = alloc(512, F32); rs_t = Tl()
        qn = alloc(2 * 512, BF16).rearrange("p (j d) -> p j d", j=2); qn_t = Tl()
        PT = alloc(2 * 512, BF16).rearrange("p (j d) -> p j d", j=2); PT_t = Tl()
        rden = alloc(512, F32); rden_t = Tl()

        kb.dma(memx, mem_in[s].rearrange("(j p) d -> p j d", p=128), writes=[memx_t])
        kb.dma(wq, win_v[:, :, QM0:QM0 + 1024], writes=[wq_t])
        rmsnorm_to_T(memx, memx_t, 2, G_MEMN, mT, mT_t, 0, (ss[0], ss[1], junk[0], junk[1], xn_rot, pst_rot))
        wkv_v = wb["w_mem_kv"].rearrange("(c p) n -> p c n", p=128)
        for blk in range(2):
            wk, wk_t = wkv_rot.next()
            kb.dma(wk, wkv_v[:, :, blk * 512:(blk + 1) * 512], writes=[wk_t])
            for hl in range(2):
                h = blk * 2 + hl
                pk, pk_t = bk[1]
                kb.deps("pe", [wk_t, mT_t], [pk_t])
                inst = None
                for dc in range(2):
                    for c in range(8):
                        inst = kb.mm_raw(pk[:, dc * 256:(dc + 1) * 256], wk[:, c, hl * 256 + dc * 128: hl * 256 + (dc + 1) * 128],
                                         mT[:, c, :], c == 0, c == 7)
                kb.mm_done(inst, [wk_t, mT_t], [pk_t])
                sq, sq_t = sq_rot.next()
                kb.op("act", lambda e: e.activation(out=sq, in_=pk, func=AF.Square), reads=[pk_t], writes=[sq_t])
                pss, pss_t = bk[2]
                kb.mm(pss_t, pss[:, 0:256], [(onesb, sq[:, 0:256]), (onesb, sq[:, 256:512])], [onesb_t, sq_t])
                kb.op("act", lambda e: e.activation(out=rs[:, 0:256], in_=pss[:, 0:256], func=AF.Ln, bias=EPS, scale=1.0 / 256),
                      reads=[pss_t], writes=[rs_t])
                kb.op("act", lambda e: e.activation(out=rs[:, 0:256], in_=rs[:, 0:256], func=AF.Exp, scale=-0.5), writes=[rs_t])
                for dc in range(2):
                    kb.op("dve", lambda e: e.scalar_tensor_tensor(out=kTn[:, h * 2 + dc, :], in0=pk[:, dc * 256:(dc + 1) * 256],
                                                                  scalar=gains[:, G_MEMK + dc:G_MEMK + dc + 1], in1=rs[:, 0:256],
                                                                  op0=ALU.mult, op1=ALU.mult),
                          reads=[pk_t, rs_t, gains_t], writes=[kTn_t])
        for blk in range(2):
            wk, wk_t = wkv_rot.next()
            kb.dma(wk, wkv_v[:, :, 1024 + blk * 512: 1024 + (blk + 1) * 512], writes=[wk_t])
            for mc in range(2):
                pv, pv_t = bk[3 + mc]
                kb.mm(pv_t, pv, [(mT[:, c, mc * 128:(mc + 1) * 128], wk[:, c, :]) for c in range(8)], [mT_t, wk_t])
                kb.op("act", lambda e: e.activation(out=vm[:, mc, blk * 512:(blk + 1) * 512], in_=pv, func=AF.Copy),
                      reads=[pv_t], writes=[vm_t])
        for t in range(4):
            tsl = slice(t * 512, (t + 1) * 512)
            for h in range(4):
                pq = [bk[0], bk[1]]
                for dc in range(2):
                    kb.mm(pq[dc][1], pq[dc][0], [(wq[:, c, h * 256 + dc * 128: h * 256 + (dc + 1) * 128], uT[:, c, tsl])
                                                 for c in range(8)], [wq_t, uT_t[t]])
                sqs = []
                for dc in range(2):
                    sq, sq_t = sq_rot.next()
                    kb.op("act", lambda e: e.activation(out=sq, in_=pq[dc][0], func=AF.Square), reads=[pq[dc][1]], writes=[sq_t])
                    sqs.append((sq, sq_t))
                pss, pss_t = bk[2]
                kb.mm(pss_t, pss, [(onesb, sqs[0][0]), (onesb, sqs[1][0])], [onesb_t, sqs[0][1], sqs[1][1]])
                kb.op("act", lambda e: e.activation(out=rs, in_=pss, func=AF.Ln, bias=EPS, scale=1.0 / 256),
                      reads=[pss_t], writes=[rs_t])
                kb.op("act", lambda e: e.activation(out=rs, in_=rs, func=AF.Exp, scale=-0.5), writes=[rs_t])
                for dc in range(2):
                    kb.op("dve", lambda e: e.scalar_tensor_tensor(out=qn[:, dc, :], in0=pq[dc][0],
                                                                  scalar=gains[:, G_MEMQ + dc:G_MEMQ + dc + 1], in1=rs,
                                                                  op0=ALU.mult, op1=ALU.mult),
                          reads=[pq[dc][1], rs_t, gains_t], writes=[qn_t])
                for mc in range(2):
                    pS, pS_t = bk[3 + mc]
                    kb.mm(pS_t, pS, [(kTn[:, h * 2 + dc, mc * 128:(mc + 1) * 128], qn[:, dc, :]) for dc in range(2)], [kTn_t, qn_t])
                    kb.op("act", lambda e: e.activation(out=PT[:, mc, :], in_=pS, func=AF.Exp, scale=1.0 / 16.0),
                          reads=[pS_t], writes=[PT_t])
                pden, pden_t = bk[7]
                kb.mm(pden_t, pden, [(onesb, PT[:, mc, :]) for mc in range(2)], [onesb_t, PT_t])
                kb.op("dve", lambda e: e.reciprocal(rden, pden), reads=[pden_t], writes=[rden_t])
                for dc in range(2):
                    po, po_t = bk[5 + dc]
                    kb.mm(po_t, po, [(vm[:, mc, h * 256 + dc * 128: h * 256 + (dc + 1) * 128], PT[:, mc, :]) for mc in range(2)],
                          [vm_t, PT_t])
                    kb.op("dve", lambda e: e.tensor_tensor(out=oT[:, h * 2 + dc, tsl], in0=po, in1=rden, op=ALU.mult),
                          reads=[po_t, rden_t], writes=[oT_t])
        L["mp"] = Rot([bk[0], bk[1], bk[2], bk[3]])
        merge_branch("mem", "w_br_mem", 0, oT, oT_t, first, L)

    def na_pat(i):
        if i == 0:
            return (0, 4, 0)
        if i == 1:
            return (4, 4, 0)
        if i == 14:
            return (13, 4, 12)
        if i == 15:
            return (17, 4, 12)
        return (8, 5, i - 2)

    def na_branch(s, first):
        nonlocal off
        off = PH0
        need("w_in"); need("nab")
        oT = alloc(8 * T, BF16).rearrange("p (c t) -> p c t", c=8)
        oT_t = Tl()
        L = merge_layout()
        blk1 = alloc(128, BF16); blk1_t = Tl()
        kb.op("pool", lambda e: e.memset(blk1, 0.0), writes=[blk1_t])
        kb.op("pool", lambda e: e.memset(blk1[0:64, 0:64], 1.0), writes=[blk1_t])
        kb.op("pool", lambda e: e.memset(blk1[64:128, 64:128], 1.0), writes=[blk1_t])
        w_rot = Rot([tuple((alloc(8 * 128, BF16).rearrange("p (c n) -> p c n", c=8), Tl()) for _ in range(3)) for _ in range(2)])
        qk_rot = Rot([((alloc(T, BF16), Tl()), (alloc(T, BF16), Tl())) for _ in range(2)])
        va_rot = []
        for _ in range(2):
            va = alloc(16 * 2 * 128, BF16).rearrange("p (j h d) -> p j h d", j=16, h=2)
            va_t = Tl()
            kb.op("pool", lambda e: e.memset(va[:, :, :, 64:128], 1.0), writes=[va_t])
            va_rot.append((va, va_t))
        va_rot = Rot(va_rot)
        nb_rot = Rot([(alloc(NA_NSLOT * 128, BF16), Tl()) for _ in range(2)])
        PT_rot = Rot([(alloc(768, BF16), Tl()) for _ in range(3)])
        sq_rot = Rot([(alloc(512, BF16), Tl()) for _ in range(2)])
        rs_rot = Rot([(alloc(512, F32), Tl()) for _ in range(2)])
        rden_rot = Rot([(alloc(128, F32), Tl()) for _ in range(2)])
        pS_rot = Rot([(kb.bank(0, 2), Tl(psum=True)), (kb.bank(2, 2), Tl(psum=True))])
        ring = [(kb.bank(4 + b), Tl(psum=True)) for b in range(4)]
        po_rot = Rot([ring[0], ring[1]])
        pq_rot = Rot([ring[2], (kb.bank(0), pS_rot.tiles[0][1]), (kb.bank(2), pS_rot.tiles[1][1])])
        pss_b = ring[3]
        pss_rot = Rot([pss_b, po_rot.tiles[0], po_rot.tiles[1]])
        pv_rot = Rot([(kb.bank(6), pq_rot.tiles[0][1]), (kb.bank(7), pss_b[1])])

        for hp in range(8):
            (wq, wq_t), (wk, wk_t), (wv, wv_t) = w_rot.next()
            kb.dma(wq, win_v[:, :, Q0 + hp * 128: Q0 + (hp + 1) * 128], writes=[wq_t])
            kb.dma(wk, win_v[:, :, K0 + hp * 128: K0 + (hp + 1) * 128], writes=[wk_t])
            kb.dma(wv, win_v[:, :, V0 + hp * 128: V0 + (hp + 1) * 128], writes=[wv_t])
            (qT, qT_t), (kT, kT_t) = qk_rot.next()
            jobs = [(w_, w_t_, dst_, dst_t_, gcol_, sc_, bs_, t_)
                    for (w_, w_t_, dst_, dst_t_, gcol_, sc_, bs_) in ((wq, wq_t, qT, qT_t, G_NAQ, 1.0, 64 * EPS), (wk, wk_t, kT, kT_t, G_NAK, 1.0 / 64, EPS))
                    for t_ in range(4)]

            def proj(job):
                w_, w_t_, dst_, dst_t_, gcol_, sc_, bs_, t_ = job
                pq, pq_t = pq_rot.next()
                kb.mm(pq_t, pq, [(w_[:, c, :], uT[:, c, t_ * 512:(t_ + 1) * 512]) for c in range(8)], [w_t_, uT_t[t_]])
                return pq, pq_t

            cur = proj(jobs[0])
            for ji, job in enumerate(jobs):
                w_, w_t_, dst_, dst_t_, gcol_, sc_, bs_, t_ = job
                pq, pq_t = cur
                if ji + 1 < len(jobs):
                    cur = proj(jobs[ji + 1])
                tsl = slice(t_ * 512, (t_ + 1) * 512)
                sq, sq_t = sq_rot.next()
                kb.op("act", lambda e: e.activation(out=sq, in_=pq, func=AF.Square), reads=[pq_t], writes=[sq_t])
                pss, pss_t = pss_rot.next()
                kb.mm(pss_t, pss, [(blk1, sq)], [blk1_t, sq_t])
                rs, rs_t = rs_rot.next()
                kb.op("act", lambda e: e.activation(out=rs, in_=pss, func=AF.Sqrt, bias=bs_, scale=sc_),
                      reads=[pss_t], writes=[rs_t])
                kb.op("dve", lambda e: e.reciprocal(rs, rs), writes=[rs_t])
                kb.op("dve", lambda e: e.scalar_tensor_tensor(out=dst_[:, tsl], in0=pq, scalar=gains[:, gcol_:gcol_ + 1], in1=rs,
                                                              op0=ALU.mult, op1=ALU.mult),
                      reads=[pq_t, rs_t, gains_t], writes=[dst_t_])
            va, va_t = va_rot.next()
            for j0 in range(0, 16, 4):
                pv, pv_t = pv_rot.next()
                kb.deps("pe", [wv_t, uT_t[j0 // 4]], [pv_t])
                inst = None
                for jj in range(4):
                    j = j0 + jj
                    for c in range(8):
                        inst = kb.mm_raw(pv[:, jj * 128:(jj + 1) * 128], uT[:, c, j * 128:(j + 1) * 128], wv[:, c, :], c == 0, c == 7)
                kb.mm_done(inst, [wv_t, uT_t[j0 // 4]], [pv_t])
                kb.op("act", lambda e: e.activation(out=va[:, j0:j0 + 4, :, 0:64],
                                                    in_=pv.rearrange("p (j h d) -> p j h d", j=4, h=2), func=AF.Copy),
                      reads=[pv_t], writes=[va_t])
            for hl in range(2):
                h = hp * 2 + hl
                hb = hl * 64
                nbt, nbt_t = nb_rot.next()
                kb.dma(nbt, nab_bf[h * 128:(h + 1) * 128, :], writes=[nbt_t])

                def emit_S(j):
                    ilo, ihi = _na_blocks(j)
                    nq = ihi - ilo + 1
                    slot0 = _NA_SLOT[j]
                    pS, pS_t = pS_rot.next()
                    kb.deps("pe", [kT_t, qT_t, nbt_t, identb_t], [pS_t])
                    inst = None
                    for (c0, c1) in ((0, min(nq, 4)), (4, nq)):
                        if c1 <= c0:
                            continue
                        kb.mm_raw(pS[:, c0 * 128:c1 * 128], kT[hb:hb + 64, j * 128:(j + 1) * 128],
                                  qT[hb:hb + 64, (ilo + c0) * 128:(ilo + c1) * 128], True, False)
                        inst = kb.mm_raw(pS[:, c0 * 128:c1 * 128], identb, nbt[:, (slot0 + c0) * 128:(slot0 + c1) * 128], False, True)
                    kb.mm_done(inst, [kT_t, qT_t, nbt_t, identb_t], [pS_t])
                    PT, PT_t = PT_rot.next()
                    kb.op("act", lambda e: e.activation(out=PT[:, 0:nq * 128], in_=pS[:, 0:nq * 128], func=AF.Exp),
                          reads=[pS_t], writes=[PT_t])
                    return PT, PT_t

                def finish_block(i):
                    rb, rb_t = ring[(i % 8) // 2]
                    col = (i % 2) * 128
                    rden, rden_t = rden_rot.next()
                    kb.op("dve", lambda e: e.reciprocal(rden[0:64, :], rb[64:128, col:col + 128]), reads=[rb_t], writes=[rden_t])
                    kb.op("dve", lambda e: e.tensor_tensor(out=oT[hb:hb + 64, hp, i * 128:(i + 1) * 128], in0=rb[0:64, col:col + 128],
                                                           in1=rden[0:64, :], op=ALU.mult),
                          reads=[rb_t, rden_t], writes=[oT_t])

                cur = emit_S(0)
                done_prev = []
                for j in range(16):
                    nxt = emit_S(j + 1) if j + 1 < 16 else None
                    PT, PT_t = cur
                    ilo, ihi = _na_blocks(j)
                    groups = []
                    for i in range(ilo, ihi + 1):
                        lo, nk_ = _na_chunks(i)
                        key = ((i % 8) // 2, (j == lo) and (i % 2 == 0), j == lo + nk_ - 1)
                        if groups and groups[-1][0] == key and groups[-1][2] == i - 1:
                            groups[-1][2] = i
                        else:
                            groups.append([key, i, i])
                    banks_w = sorted(set(g_[0][0] for g_ in groups))
                    kb.deps("pe", [va_t, PT_t], [ring[b_][1] for b_ in banks_w])
                    inst = None
                    for (bank_, st_, sp_), i0, i1 in groups:
                        rb, rb_t = ring[bank_]
                        inst = kb.mm_raw(rb[:, (i0 % 2) * 128:(i1 % 2 + 1) * 128], va[:, j, hl, :],
                                         PT[:, (i0 - ilo) * 128:(i1 - ilo + 1) * 128], st_, sp_, skip=True)
                    kb.mm_done(inst, [va_t, PT_t], [ring[b_][1] for b_ in banks_w])
                    for i in done_prev:
                        finish_block(i)
                    done_prev = [i for i in range(ilo, ihi + 1) if j == _na_chunks(i)[0] + _na_chunks(i)[1] - 1]
                    cur = nxt
                for i in done_prev:
                    finish_block(i)
        L["mp"] = Rot([(kb.bank(0), pS_rot.tiles[0][1]), (kb.bank(2), pS_rot.tiles[1][1]), po_rot.tiles[0], po_rot.tiles[1]])
        merge_branch("na", "w_br_na", 0, oT, oT_t, first, L)

    def ssd_branch(s, first):
        nonlocal off
        off = PH0
        need("w_in")
        oT = alloc(4 * T, BF16).rearrange("p (c t) -> p c t", c=4)
        oT_t = Tl()
        L = merge_layout()
        sc = alloc(352, F32); sc_t = Tl()
        kb.dma(sc, ssdc_in, writes=[sc_t])
        convw = sc[:, 0:160].rearrange("p (c k) -> p c k", k=5)
        convb = sc[:, 160:192]
        tri = alloc(256, F32); tri_t = Tl()
        kb.dma(tri, tri_in[:, 0:256], writes=[tri_t])
        triU = tri[:, 0:128]; triL = tri[:, 128:256]
        mk = alloc(1024, BF16).rearrange("p (d r l) -> p d r l", d=2, r=4); mk_t = Tl()
        for d in range(2):
            for r in range(4):
                kb.dma(mk[:, d, r, :], tri_in[:, 256 + d * 128: 256 + (d + 1) * 128], writes=[mk_t], q="pool")
        onesf = alloc(128, F32); onesf_t = Tl()
        kb.op("pool", lambda e: e.memset(onesf, 1.0), writes=[onesf_t])
        wdt = alloc(8 * 64, BF16).rearrange("p (c n) -> p c n", c=8); wdt_t = Tl()
        kb.dma(wdt, win_v[:, :, DT0:DT0 + 64], writes=[wdt_t])
        w_rot = Rot([(alloc(8 * 128, BF16).rearrange("p (c n) -> p c n", c=8), Tl()) for _ in range(2)])
        wz = alloc(8 * 256, BF16).rearrange("p (c n) -> p c n", c=8); wz_t = Tl()
        pre_l = []
        for _ in range(2):
            pre = alloc(2052, BF16); pre_t = Tl()
            kb.op("pool", lambda e: e.memset(pre[:, 0:2], 0.0), writes=[pre_t])
            kb.op("pool", lambda e: e.memset(pre[:, 2050:2052], 0.0), writes=[pre_t])
            pre_l.append((pre, pre_t))
        pre_rot = Rot(pre_l)
        dg_rot = Rot([(alloc(5 * 128, BF16).rearrange("p (k n) -> p k n", k=5), Tl()) for _ in range(2)])
        xfm_rot = Rot([(alloc(2048, BF16), Tl()) for _ in range(2)])
        Bfm = alloc(2048, BF16); Bfm_t = Tl()
        Cfm = alloc(2048, BF16); Cfm_t = Tl()
        xtm = alloc(16 * 256, BF16).rearrange("p (j d) -> p j d", j=16); xtm_t = Tl()
        Btm = alloc(16 * 128, BF16).rearrange("p (j d) -> p j d", j=16); Btm_t = Tl()

        def small():
            return alloc(128, F32).rearrange("p (j h) -> p j h", j=16), Tl()
        dt, dt_t = small(); lndt, lndt_t = small(); dA, dA_t = small(); Ac, Ac_t = small(); tot, tot_t = small()
        nb, nb_t = small(); wS, wS_t = small(); eA, eA_t = small(); cd, cd_t = small()
        a8 = alloc(8, F32); a8_t = Tl()
        b8 = alloc(8, F32); b8_t = Tl()
        d4 = alloc(4, F32); d4_t = Tl()
        identD = alloc(4 * 128, BF16).rearrange("p (r n) -> p r n", r=4); identD_t = Tl()
        prev = [(alloc(16 * 256, BF16).rearrange("p (j d) -> p j d", j=16), Tl()) for _ in range(2)]
        Hs = [(alloc(256, F32), Tl()) for _ in range(2)]
        xwas = [(alloc(16 * 256, BF16).rearrange("p (j d) -> p j d", j=16), Tl()) for _ in range(2)]
        Gt_rot = Rot([(alloc(128, F32), Tl()) for _ in range(2)])
        E_rot = Rot([((alloc(512, F32), Tl()), (alloc(512, F32), Tl())) for _ in range(2)])
        Mt_rot = Rot([(alloc(512, BF16).rearrange("p (r l) -> p r l", r=4), Tl()) for _ in range(2)])
        sz_rot = Rot([(alloc(256, F32), Tl()) for _ in range(3)])
        t2_rot = Rot([(alloc(256, F32), Tl()) for _ in range(2)])
        yn_rot = Rot([(alloc(256, BF16), Tl()) for _ in range(2)])
        ssq = alloc(2, F32); ssq_t = Tl()
        t12 = alloc(512, F32); t12_t = Tl()
        SEG_rot = Rot([((kb.bank(0), Tl(psum=True)), (kb.bank(1), Tl(psum=True))), ((kb.bank(2), Tl(psum=True)), (kb.bank(3), Tl(psum=True)))])
        pZT = kb.bank(4); pZT_t = Tl(psum=True)
        pYb = kb.bank(5); pY_t = Tl(psum=True)
        pO = kb.bank(6); pO_t = Tl(psum=True)
        pst = kb.bank(7); pst_t = Tl(psum=True)
        pGb = pst; pG_t = pst_t
        pst_rot = Rot([(kb.bank(4), pZT_t), (kb.bank(5), pY_t), (kb.bank(6), pO_t), (kb.bank(7), pst_t)])
        pZ_t = pZT_t; pT_t = pZT_t
        pp_rot = Rot([(kb.bank(0), SEG_rot.tiles[0][0][1]), (kb.bank(1), SEG_rot.tiles[0][1][1])])
        pc_rot = Rot([(kb.bank(2), SEG_rot.tiles[1][0][1]), (kb.bank(3), SEG_rot.tiles[1][1][1])])
        ptr_rot = Rot([(kb.bank_bf(4), pZT_t), (kb.bank_bf(5), pY_t)])

        for g in range(8):
            kb.dma(wz, win_v[:, :, Z0 + g * 256: Z0 + (g + 1) * 256], writes=[wz_t])
            def s1chunk(ci, cc):
                w, w_t = w_rot.next()
                kb.dma(w, win_v[:, :, X0 + cc * 128: X0 + (cc + 1) * 128], writes=[w_t])
                pre, pre_t = pre_rot.next()
                dg, dg_t = dg_rot.next()
                kb.op("dve", lambda e: e.tensor_tensor(out=dg, in0=bc(identf.unsqueeze(1), [P, 5, 128]),
                                                       in1=bc(convw[:, cc, :].unsqueeze(2), [P, 5, 128]), op=ALU.mult),
                      reads=[identf_t, sc_t], writes=[dg_t])
                for t in range(4):
                    pp, pp_t = pp_rot.next()
                    kb.mm(pp_t, pp, [(w[:, c, :], uT[:, c, t * 512:(t + 1) * 512]) for c in range(8)], [w_t, uT_t[t]])
                    kb.op("act", lambda e: e.activation(out=pre[:, 2 + t * 512: 2 + (t + 1) * 512], in_=pp, func=AF.Copy),
                          reads=[pp_t], writes=[pre_t])
                if ci < 2:
                    dst, dst_t = xfm_rot.next()
                elif ci == 2:
                    dst, dst_t = Bfm, Bfm_t
                else:
                    dst, dst_t = Cfm, Cfm_t
                for t in range(4):
                    pc, pc_t = pc_rot.next()
                    kb.mm(pc_t, pc, [(dg[:, k, :], pre[:, t * 512 + k: t * 512 + k + 512]) for k in range(5)], [dg_t, pre_t])
                    kb.op("act", lambda e: e.activation(out=dst[:, t * 512:(t + 1) * 512], in_=pc, func=AF.Silu, bias=convb[:, cc:cc + 1]),
                          reads=[pc_t, sc_t], writes=[dst_t])
                if ci < 3:
                    for j0 in (0, 8):
                        ptr, ptr_t = ptr_rot.next()
                        kb.deps("pe", [dst_t, identb_t], [ptr_t])
                        inst = None
                        for jj in range(8):
                            inst = kb.tr_raw(ptr[:, jj * 128:(jj + 1) * 128], dst[:, (j0 + jj) * 128:(j0 + jj + 1) * 128], identb)
                        kb.mm_done(inst, [dst_t, identb_t], [ptr_t])
                        if ci < 2:
                            o_ap, o_t = xtm[:, j0:j0 + 8, ci * 128:(ci + 1) * 128], xtm_t
                        else:
                            o_ap, o_t = Btm[:, j0:j0 + 8, :], Btm_t
                        kb.op("dve", lambda e: e.tensor_copy(out=o_ap, in_=ptr.rearrange("p (j d) -> p j d", j=8)),
                              reads=[ptr_t], writes=[o_t])
            def s2a():
                kb.op("pool", lambda e: e.tensor_copy(out=b8.rearrange("p (d r) -> p d r", d=2),
                                                      in_=sc[:, 192:256].rearrange("p (d h) -> p d h", d=2)[:, :, 4 * g:4 * g + 4]),
                      reads=[sc_t], writes=[b8_t])
                kb.op("act", lambda e: e.activation(out=a8.rearrange("p (d r) -> p d r", d=2),
                                                    in_=sc[:, 256:320].rearrange("p (d h) -> p d h", d=2)[:, :, 4 * g:4 * g + 4], func=AF.Exp),
                      reads=[sc_t], writes=[a8_t])
                kb.op("dve", lambda e: e.tensor_scalar(out=a8, in0=a8, scalar1=-1.0, scalar2=None, op0=ALU.mult), writes=[a8_t])
                for r in range(4):
                    kb.op("dve", lambda e: e.tensor_scalar(out=identD[:, r, :], in0=identf, scalar1=sc[:, 320 + 4 * g + r: 321 + 4 * g + r],
                                                           scalar2=None, op0=ALU.mult), reads=[identf_t, sc_t], writes=[identD_t])
                pdt, pdt_t = pp_rot.next()
                kb.deps("pe", [wdt_t] + uT_t, [pdt_t])
                inst = None
                wdt_g = wdt.rearrange("p c (d h) -> p c d h", d=2)
                for j in range(16):
                    for c in range(8):
                        inst = kb.mm_raw(pdt[:, j * 8:(j + 1) * 8], uT[:, c, j * 128:(j + 1) * 128], wdt_g[:, c, :, 4 * g:4 * g + 4], c == 0, c == 7)
                kb.mm_done(inst, [wdt_t] + uT_t, [pdt_t])
                j3 = lambda ap: ap.rearrange("p (j h) -> p j h", j=16)
                b8b = bc(b8.unsqueeze(1), [P, 16, 8])
                a8b = bc(a8.unsqueeze(1), [P, 16, 8])
                kb.op("dve", lambda e: e.tensor_tensor(out=dt, in0=j3(pdt[:, 0:128]), in1=b8b, op=ALU.add), reads=[pdt_t, b8_t], writes=[dt_t])
                kb.op("act", lambda e: e.activation(out=dt, in_=dt, func=AF.Exp), writes=[dt_t])
                kb.op("act", lambda e: e.activation(out=dt, in_=dt, func=AF.Ln, bias=1.0), writes=[dt_t])
                kb.op("act", lambda e: e.activation(out=lndt, in_=dt, func=AF.Ln), reads=[dt_t], writes=[lndt_t])
                kb.op("dve", lambda e: e.tensor_tensor(out=dA, in0=dt, in1=a8b, op=ALU.mult), reads=[dt_t, a8_t], writes=[dA_t])
            def s2b():
                j3 = lambda ap: ap.rearrange("p (j h) -> p j h", j=16)
                pcs, pcs_t = pp_rot.next()
                kb.deps("pe", [dA_t, tri_t, onesf_t], [pcs_t])
                inst = None
                for j in range(16):
                    kb.mm_raw(pcs[:, j * 8:j * 8 + 4], triU, dA[:, j, 0:4], True, True)
                    kb.mm_raw(pcs[:, j * 8 + 4:j * 8 + 8], triL, dA[:, j, 4:8], True, True)
                    inst = kb.mm_raw(pcs[:, 128 + j * 8:128 + (j + 1) * 8], onesf, dA[:, j, :], True, True)
                kb.mm_done(inst, [dA_t, tri_t, onesf_t], [pcs_t])
                kb.op("act", lambda e: e.activation(out=Ac, in_=j3(pcs[:, 0:128]), func=AF.Copy), reads=[pcs_t], writes=[Ac_t])
                kb.op("act", lambda e: e.activation(out=tot, in_=j3(pcs[:, 128:256]), func=AF.Copy), reads=[pcs_t], writes=[tot_t])
                kb.op("dve", lambda e: e.tensor_tensor(out=nb, in0=lndt, in1=Ac, op=ALU.subtract), reads=[lndt_t, Ac_t], writes=[nb_t])
                kb.op("dve", lambda e: e.tensor_tensor(out=wS, in0=tot, in1=nb, op=ALU.add), reads=[tot_t, nb_t], writes=[wS_t])
                kb.op("act", lambda e: e.activation(out=wS, in_=wS, func=AF.Exp), writes=[wS_t])
                kb.op("act", lambda e: e.activation(out=eA, in_=Ac, func=AF.Exp), reads=[Ac_t], writes=[eA_t])
                kb.op("act", lambda e: e.activation(out=cd, in_=tot, func=AF.Exp), reads=[tot_t], writes=[cd_t])
            chs = (2 * g, 2 * g + 1, 16 + g, 24 + g)
            s2a()
            s1chunk(0, chs[0])
            s2b()
            for ci_ in range(1, 4):
                s1chunk(ci_, chs[ci_])
            orders = (list(range(16)), list(range(15, -1, -1)))
            for d in range(2):
                pv_, pv_t = prev[d]
                kb.op("pool", lambda e: e.memset(pv_[:, orders[d][0], :], 0.0), writes=[pv_t])
                xwa, xwa_t = xwas[d]
                kb.op("dve", lambda e: e.tensor_tensor(out=xwa.rearrange("p j (r q) -> p j r q", r=4),
                                                       in0=xtm.rearrange("p j (r q) -> p j r q", r=4),
                                                       in1=bc(wS[:, :, 4 * d:4 * d + 4].unsqueeze(3), [P, 16, 4, 64]), op=ALU.mult),
                      reads=[xtm_t, wS_t], writes=[xwa_t])
            for n_ in range(15):
                for d in range(2):
                    pv_, pv_t = prev[d]
                    H, H_t = Hs[d]
                    xwa, xwa_t = xwas[d]
                    c = orders[d][n_]
                    ps_, ps_t = pst_rot.next()
                    kb.mm(ps_t, ps_[:, 0:256], [(Btm[:, c, :], xwa[:, c, :])], [Btm_t, xwa_t])
                    if n_ == 0:
                        kb.op("dve", lambda e: e.tensor_copy(out=H, in_=ps_[:, 0:256]), reads=[ps_t], writes=[H_t])
                    else:
                        kb.op("dve", lambda e: e.tensor_tensor(out=H.rearrange("p (r q) -> p r q", r=4), in0=H.rearrange("p (r q) -> p r q", r=4),
                                                               in1=bc(cd[:, c, 4 * d:4 * d + 4].unsqueeze(2), [P, 4, 64]), op=ALU.mult),
                              reads=[cd_t], writes=[H_t])
                        kb.op("dve", lambda e: e.tensor_tensor(out=H, in0=H, in1=ps_[:, 0:256], op=ALU.add), reads=[ps_t], writes=[H_t])
                    kb.op("act", lambda e: e.activation(out=pv_[:, orders[d][n_ + 1], :], in_=H, func=AF.Copy), reads=[H_t], writes=[pv_t])
            keep = {}
            r4 = lambda ap: ap.rearrange("p (r q) -> p r q", r=4)

            keepA = {}

            def stA1(c):
                csl = slice(c * 128, (c + 1) * 128)
                kb.mm(pG_t, pGb[:, 0:128], [(Bfm[:, csl], Cfm[:, csl])], [Bfm_t, Cfm_t])
                kb.mm(pZ_t, pZT[:, 0:256], [(uT[:, cc_, csl], wz[:, cc_, :]) for cc_ in range(8)], [uT_t[c // 4], wz_t])
                Gt, Gt_t = Gt_rot.next()
                kb.op("act", lambda e: e.activation(out=Gt, in_=pGb[:, 0:128], func=AF.Copy), reads=[pG_t], writes=[Gt_t])
                sz, sz_t = sz_rot.next()
                kb.op("act", lambda e: e.activation(out=sz, in_=pZT[:, 0:256], func=AF.Exp, scale=-1.0), reads=[pZ_t], writes=[sz_t])
                segs = SEG_rot.next()
                Es = E_rot.next()
                for d in range(2):
                    sg_, sg_t = segs[d]
                    kb.deps("pe", [identb_t, mk_t, dA_t, tri_t], [sg_t])
                    kb.mm_raw(sg_, identb, mk[:, d].rearrange("p r l -> p (r l)"), True, False)
                    inst = None
                    for r in range(4):
                        inst = kb.mm_raw(sg_[:, r * 128:(r + 1) * 128], bc(dA[:, c, 4 * d + r:4 * d + r + 1], [P, 128]),
                                         triU if d == 0 else triL, False, True)
                    kb.mm_done(inst, [identb_t, mk_t, dA_t, tri_t], [sg_t])
                    E, E_t = Es[d]
                    for r in range(4):
                        kb.op("act", lambda e: e.activation(out=E[:, r * 128:(r + 1) * 128], in_=sg_[:, r * 128:(r + 1) * 128], func=AF.Exp,
                                                            bias=nb[:, c, 4 * d + r:4 * d + r + 1]),
                              reads=[sg_t, nb_t], writes=[E_t])
                kb.op("pool", lambda e: e.tensor_scalar(out=sz, in0=sz, scalar1=1.0, scalar2=None, op0=ALU.add), writes=[sz_t])
                kb.op("dve", lambda e: e.reciprocal(sz, sz), writes=[sz_t])
                kb.op("dve", lambda e: e.tensor_tensor(out=sz, in0=pZT[:, 0:256], in1=sz, op=ALU.mult), reads=[], writes=[sz_t, pZ_t])
                keepA[c] = (Es, Gt, Gt_t, sz, sz_t)

            def stA2(c):
                Es, Gt, Gt_t, sz, sz_t = keepA.pop(c)
                (Ef, Ef_t), (Eb, Eb_t) = Es
                kb.op("dve", lambda e: e.tensor_tensor(out=Ef, in0=Ef, in1=Eb, op=ALU.add), reads=[Eb_t], writes=[Ef_t])
                Mt, Mt_t = Mt_rot.next()
                kb.op("dve", lambda e: e.tensor_tensor(out=Mt, in0=Ef.rearrange("p (r l) -> p r l", r=4),
                                                       in1=bc(Gt.unsqueeze(1), [P, 4, 128]), op=ALU.mult),
                      reads=[Ef_t, Gt_t], writes=[Mt_t])
                keep[c] = (Mt, Mt_t, sz, sz_t)

            def stB(c):
                csl = slice(c * 128, (c + 1) * 128)
                Mt, Mt_t, sz, sz_t = keep[c]
                kb.deps("pe", [Mt_t, xtm_t, identD_t], [pY_t])
                inst = None
                for r in range(4):
                    kb.mm_raw(pYb[:, r * 64:(r + 1) * 64], Mt[:, r, :], xtm[:, c, r * 64:(r + 1) * 64], True, False)
                    inst = kb.mm_raw(pYb[:, r * 64:(r + 1) * 64], identD[:, r, :], xtm[:, c, r * 64:(r + 1) * 64], False, True)
                kb.mm_done(inst, [Mt_t, xtm_t, identD_t], [pY_t])
                kb.deps("pe", [Cfm_t, prev[0][1], prev[1][1]], [pO_t])
                kb.mm_raw(pO[:, 0:256], Cfm[:, csl], prev[0][0][:, c, :], True, True)
                inst = kb.mm_raw(pO[:, 256:512], Cfm[:, csl], prev[1][0][:, c, :], True, True)
                kb.mm_done(inst, [Cfm_t, prev[0][1], prev[1][1]], [pO_t])
                t2, t2_t = t2_rot.next()
                kb.op("dve", lambda e: e.tensor_tensor(out=t12.rearrange("p (a q) -> p a q", a=8), in0=pO.rearrange("p (a q) -> p a q", a=8),
                                                       in1=bc(eA[:, c, :].unsqueeze(2), [P, 8, 64]), op=ALU.mult),
                      reads=[pO_t, eA_t], writes=[t12_t])
                kb.op("dve", lambda e: e.tensor_tensor(out=t12[:, 0:256], in0=t12[:, 0:256], in1=t12[:, 256:512], op=ALU.add), writes=[t12_t])
                kb.op("dve", lambda e: e.tensor_tensor(out=t2, in0=pYb[:, 0:256], in1=t12[:, 0:256], op=ALU.add), reads=[pY_t, t12_t], writes=[t2_t])
                kb.op("pool", lambda e: e.tensor_tensor(out=t2, in0=t2, in1=sz, op=ALU.mult), reads=[sz_t], writes=[t2_t])
                keep[c] = (t2, t2_t)

            def stB2(c):
                t2, t2_t = keep[c]
                kb.op("dve", lambda e: e.memset(ssq[:, 0:1], 0.0), writes=[ssq_t])
                yn, yn_t = yn_rot.next()
                kb.op("act", lambda e: e.activation(out=yn, in_=t2, func=AF.Square, accum_out=ssq[:, 0:1]), reads=[t2_t], writes=[yn_t, ssq_t])
                kb.op("act", lambda e: e.activation(out=ssq[:, 0:1], in_=ssq[:, 0:1], func=AF.Ln, bias=EPS, scale=1.0 / 256), writes=[ssq_t])
                kb.op("act", lambda e: e.activation(out=ssq[:, 0:1], in_=ssq[:, 0:1], func=AF.Exp, scale=-0.5), writes=[ssq_t])
                kb.op("act", lambda e: e.activation(out=yn, in_=t2, func=AF.Copy, scale=ssq[:, 0:1]),
                      reads=[t2_t, ssq_t], writes=[yn_t])
                keep[c] = (yn, yn_t)

            def stC(c):
                csl = slice(c * 128, (c + 1) * 128)
                yn, yn_t = keep.pop(c)
                pTb = pZT[:, 256:384].bitcast(BF16)
                kb.deps("pe", [yn_t, identb_t], [pT_t])
                kb.tr_raw(pTb[:, 0:128], yn[:, 0:128], identb)
                inst = kb.tr_raw(pTb[:, 128:256], yn[:, 128:256], identb)
                kb.mm_done(inst, [yn_t, identb_t], [pT_t])
                oc = (2 * g) % 4
                kb.op("dve", lambda e: e.tensor_tensor(out=oT[:, oc:oc + 2, csl], in0=pTb.rearrange("p (k t) -> p k t", k=2),
                                                       in1=bc(gains[:, G_SSDN + 2 * g: G_SSDN + 2 * g + 2].unsqueeze(2), [P, 2, 128]), op=ALU.mult),
                      reads=[pT_t, gains_t], writes=[oT_t])
            for i in range(21):
                if 0 <= i - 1 < 16:
                    stA2(i - 1)
                if i < 16:
                    stA1(i)
                if 0 <= i - 2 < 16:
                    stB(i - 2)
                if 0 <= i - 3 < 16:
                    stB2(i - 3)
                if 0 <= i - 4 < 16:
                    stC(i - 4)
            if g % 2 == 1:
                L["mp"] = Rot([(kb.bank(0), SEG_rot.tiles[0][0][1]), (kb.bank(1), SEG_rot.tiles[0][1][1]),
                               (kb.bank(2), SEG_rot.tiles[1][0][1]), (kb.bank(3), SEG_rot.tiles[1][1][1])])
                merge_branch("ssd", "w_br_ssd", (g - 1) * 256, oT, oT_t, first, L, nk=4, gate=("store" if g == 1 else "load"))
                first = False

    def phase_B(s):
        first = True
        if "ssd" in BR:
            ssd_branch(s, first)
            first = False
            kb.barrier()
        if "na" in BR:
            na_branch(s, first)
            first = False
            kb.barrier()
        if "mem" in BR:
            mem_branch(s, first)
            first = False
            kb.barrier()
        if first:
            for t in range(4):
                kb.op("pool", lambda e: e.memset(mg[:, :, t * 512:(t + 1) * 512], 0.0), writes=[mg_t[t]])

    nseq = NSEQ if debug is None else debug.get("nseq", NSEQ)
    for n_ in FAST:
        need(n_)
    for e_ in ("dve", "act", "pe"):
        for (_, t_) in stb_rot.tiles + stg_rot.tiles:
            kb.deps(e_, [], [t_])
    for s in range(nseq):
        phase_A(s)
        kb.barrier()
        phase_B(s)
        kb.barrier()
        phase_C(s)
        kb.barrier()
    kb.finish()
    return nc, kb


def _na_bias_layout(rpb):
    def rs(r):
        return min(max(r - 4, 0), 24)
    qrl = np.arange(128) // 64
    qc = np.arange(128) % 64
    krl = np.arange(128) // 64
    kc = np.arange(128) % 64
    cs = np.clip(qc - 8, 0, 48)
    col_ok = (kc[:, None] >= cs[None, :]) & (kc[:, None] < cs[None, :] + 16)
    rel_col = np.clip(kc[:, None] - qc[None, :] + 15, 0, 30)
    out = np.full((16, 128, NA_NSLOT, 128), NEG, np.float32)

    def fill(slot, j, i):
        r = 2 * i + qrl
        rsr = np.array([rs(int(v)) for v in r])
        kr = 2 * j + krl
        row_ok = (kr[:, None] >= rsr[None, :]) & (kr[:, None] < rsr[None, :] + 8)
        rel_row = np.clip(kr[:, None] - r[None, :] + 7, 0, 14)
        ok = row_ok & col_ok
        vals = rpb[:, rel_row, rel_col]
        out[:, :, slot, :] = np.where(ok[None], vals, np.float32(NEG))

    for j in (0, 1, 2, 3, 12, 13, 14, 15):
        ilo, ihi = _na_blocks(j)
        for n, i in enumerate(range(ilo, ihi + 1)):
            fill(_NA_SLOT[j] + n, j, i)
    for n, i in enumerate(range(4, 9)):
        fill(_NA_INT + n, 6, i)
    return np.ascontiguousarray(out.reshape(16 * 128, NA_NSLOT * 128))


def _host_inputs(inputs):
    xs = np.concatenate([inputs["x_prompt"], inputs["x_sample"]], axis=0)
    ms = np.concatenate([inputs["mem_prompt"], inputs["mem_sample"]], axis=0)
    common = {}
    for n in ("ffn1_w_gate", "ffn1_w_up", "ffn1_w_down", "w_in", "w_mem_kv", "w_br_na", "w_br_ssd", "w_br_mem",
              "w_out", "ffn2_w_gate", "ffn2_w_up", "ffn2_w_down"):
        common[n] = np.ascontiguousarray(inputs[n][0], dtype=np.float32)
    g = np.zeros((128, 64), np.float32)

    def col8(v):
        return np.asarray(v, np.float32).reshape(-1, 128).T

    g[:, 0:8] = col8(inputs["ffn1_norm"][0])
    g[:, 8:16] = col8(inputs["mix_norm"][0])
    g[:, 16:24] = col8(inputs["ffn2_norm"][0])
    g[:, 24:32] = col8(inputs["mem_norm"][0])
    g[:, 32] = np.tile(np.asarray(inputs["na_q_norm"][0], np.float32), 2)
    g[:, 33] = np.tile(np.asarray(inputs["na_k_norm"][0], np.float32), 2)
    g[:, 34:36] = col8(inputs["mem_q_norm"][0])
    g[:, 36:38] = col8(inputs["mem_k_norm"][0])
    g[:, 38:54] = col8(inputs["ssd_norm"][0])
    common["nab"] = _na_bias_layout(np.asarray(inputs["na_rpb"][0], np.float32))
    sc = np.zeros((128, 352), np.float32)
    cw = np.asarray(inputs["conv_w"][0], np.float32)
    sc[:, 0:160] = cw.reshape(5, 32, 128).transpose(2, 1, 0).reshape(128, 160)
    sc[:, 160:192] = np.asarray(inputs["conv_b"][0], np.float32).reshape(32, 128).T
    sc[:, 192:224] = np.asarray(inputs["dt_bias_f"][0], np.float32)[None, :]
    sc[:, 224:256] = np.asarray(inputs["dt_bias_b"][0], np.float32)[None, :]
    sc[:, 256:288] = np.asarray(inputs["a_log_f"][0], np.float32)[None, :]
    sc[:, 288:320] = np.asarray(inputs["a_log_b"][0], np.float32)[None, :]
    sc[:, 320:352] = np.asarray(inputs["ssd_d"][0], np.float32)[None, :]
    common["ssdc"] = sc
    ii = np.arange(128)
    tri = np.zeros((128, 512), np.float32)
    tri[:, 0:128] = (ii[:, None] <= ii[None, :])
    tri[:, 128:256] = (ii[:, None] >= ii[None, :])
    tri[:, 256:384] = np.where(ii[:, None] <= ii[None, :], 0.0, NEG)
    tri[:, 384:512] = np.where(ii[:, None] >= ii[None, :], 0.0, NEG)
    common["tri"] = tri
    common["gains"] = g
    common["ident"] = np.eye(128, dtype=np.float32)
    maps = []
    for i in range(8):
        m = dict(common)
        m["x"] = np.ascontiguousarray(xs[3 * i:3 * i + 3], dtype=np.float32)
        m["mem"] = np.ascontiguousarray(ms[3 * i:3 * i + 3], dtype=np.float32)
        maps.append(m)
    return maps


_CACHE = {}


def kernel(**inputs):
    if "nc" not in _CACHE:
        _CACHE["nc"] = build_program()[0]
    nc = _CACHE["nc"]
    maps = _host_inputs(inputs)
    res = run_bass_kernel_spmd(nc, maps, core_ids=list(range(8)))
    ys = np.concatenate([np.asarray(r["y"], dtype=np.float32) for r in res.results], axis=0)
    return (np.ascontiguousarray(ys[0:8]), np.ascontiguousarray(ys[8:24]))
```

```python
import numpy as np
import concourse.bass as bass
import concourse.mybir as mybir
from concourse.bass_utils import run_bass_kernel_spmd

F32 = mybir.dt.float32
BF16 = mybir.dt.bfloat16
U8 = mybir.dt.uint8
AF = mybir.ActivationFunctionType
ALU = mybir.AluOpType

D = 1024
T = 2048
NSEQ = 3
DFF = 2816
NHC = DFF // 128
INW = 13376
MEMT = 256
EPS = 1e-6
NEG = -30000.0
Q0, K0, V0, QM0, Z0, X0, B0, C0, DT0, G0 = 0, 1024, 2048, 3072, 4096, 6144, 8192, 9216, 10240, 10304

ENABLE = {"mem": True, "na": True, "ssd": True}


def _na_chunks(i):
    if i <= 1:
        return 0, 4
    if i >= 14:
        return 12, 4
    return i - 2, 5


def _na_blocks(j):
    bl = [i for i in range(16) if _na_chunks(i)[0] <= j < _na_chunks(i)[0] + _na_chunks(i)[1]]
    return bl[0], bl[-1]


_NA_SLOT = {}
_o = 0
for _j in (0, 1, 2, 3):
    _NA_SLOT[_j] = _o
    _o += _na_blocks(_j)[1] - _na_blocks(_j)[0] + 1
_NA_INT = _o
_o += 5
for _j in (12, 13, 14, 15):
    _NA_SLOT[_j] = _o
    _o += _na_blocks(_j)[1] - _na_blocks(_j)[0] + 1
for _j in range(4, 12):
    _NA_SLOT[_j] = _NA_INT
NA_NSLOT = _o


class Tok:
    __slots__ = ("sem", "val", "eng")

    def __init__(self, sem, val, eng):
        self.sem, self.val, self.eng = sem, val, eng


class Tl:
    __slots__ = ("ap", "w", "r", "psum")

    def __init__(self, ap=None, psum=False):
        self.ap = ap
        self.w = None
        self.r = {}
        self.psum = psum


class KB:
    NDS = 24

    def __init__(self, nc):
        self.nc = nc
        self.engs = {"pe": nc.tensor, "act": nc.scalar, "dve": nc.vector, "pool": nc.gpsimd, "sp": nc.sync}
        self.sem = {e: nc.alloc_semaphore("s_" + e) for e in ("pe", "act", "dve", "pool")}
        self.cnt = {e: 0 for e in self.sem}
        self.waited = {e: {} for e in self.engs}
        self.dsem = [nc.alloc_semaphore("d%d" % i) for i in range(self.NDS)]
        self.dcnt = [0] * self.NDS
        self.dlast = [None] * self.NDS
        self.di = {"sp": 0, "pool": 0}
        self.dslots = {"sp": list(range(0, 16)), "pool": list(range(16, self.NDS))}
        self.last = {}
        self.arena = nc.alloc_sbuf_tensor("arena", [128, 206 * 1024], U8)
        self.ps = nc.alloc_psum_tensor("ps", [128, 4096], F32)
        self.nops = 0

    def view(self, off, n, dt):
        sz = mybir.dt.size(dt)
        assert off % 4 == 0 and off + n * sz <= 206 * 1024, (off, n, sz)
        return self.arena[:, off:off + n * sz].bitcast(dt)

    def bank(self, b, nb=1):
        return self.ps[:, b * 512:(b + nb) * 512]

    def bank_bf(self, b):
        return self.ps[:, b * 512:(b + 1) * 512].bitcast(BF16)

    def wait(self, eng, tok):
        if tok is None:
            return
        if tok.eng == eng and eng in ("pe", "sp"):
            return
        key = id(tok.sem)
        w = self.waited[eng]
        if w.get(key, 0) >= tok.val:
            return
        self.engs[eng].wait_ge(tok.sem, tok.val)
        w[key] = tok.val

    def deps(self, eng, reads, writes):
        for t in reads:
            self.wait(eng, t.w)
            if t.psum:
                for r in t.r.values():
                    self.wait(eng, r)
        for t in writes:
            self.wait(eng, t.w)
            for r in t.r.values():
                self.wait(eng, r)

    def mark(self, tok, reads, writes):
        for t in reads:
            if t.psum:
                t.w = tok
                t.r = {}
                continue
            t.r[tok.eng if tok.eng != "dma" else id(tok.sem)] = tok
        for t in writes:
            t.w = tok
            t.r = {}

    def op(self, eng, fn, reads=(), writes=()):
        self.deps(eng, reads, writes)
        inst = fn(self.engs[eng])
        self.cnt[eng] += 1
        inst.then_inc(self.sem[eng], 1)
        tok = Tok(self.sem[eng], self.cnt[eng], eng)
        self.last[eng] = tok
        self.mark(tok, reads, writes)
        self.nops += 1
        return tok

    def mm(self, out_t, out_ap, pairs, reads, start=True, stop=True, transpose=False, ident=None):
        self.deps("pe", reads, [out_t])
        n = len(pairs)
        inst = None
        for i, (l, r) in enumerate(pairs):
            inst = self.nc.tensor.matmul(out_ap, l, r, start=(start and i == 0), stop=(stop and i == n - 1))
        self.nops += n
        return self.mm_done(inst, reads, [out_t])

    def mm_raw(self, out_ap, l, r, start, stop, skip=False):
        self.nops += 1
        if skip:
            return self.nc.tensor.matmul(out_ap, l, r, start=start, stop=stop, skip_group_check=True)
        return self.nc.tensor.matmul(out_ap, l, r, start=start, stop=stop)

    def tr_raw(self, out_ap, in_ap, ident):
        self.nops += 1
        return self.nc.tensor.transpose(out_ap, in_ap, ident)

    def mm_done(self, inst, reads, writes):
        self.cnt["pe"] += 1
        inst.then_inc(self.sem["pe"], 1)
        tok = Tok(self.sem["pe"], self.cnt["pe"], "pe")
        self.last["pe"] = tok
        self.mark(tok, reads, writes)
        return tok

    def dma(self, out_ap, in_ap, reads=(), writes=(), q="sp"):
        sl = self.dslots[q]
        i = sl[self.di[q] % len(sl)]
        self.di[q] += 1
        self.deps(q, reads, writes)
        self.wait(q, self.dlast[i])
        inst = self.engs[q].dma_start(out=out_ap, in_=in_ap)
        self.dcnt[i] += 16
        inst.then_inc(self.dsem[i], 16)
        tok = Tok(self.dsem[i], self.dcnt[i], "dma")
        self.dlast[i] = tok
        self.mark(tok, reads, writes)
        self.nops += 1
        return tok

    def barrier(self):
        toks = [t for t in self.last.values()] + [t for t in self.dlast if t is not None]
        for e in self.engs:
            for t in toks:
                self.wait(e, t)

    def finish(self):
        for t in self.dlast:
            self.wait("sp", t)
        for t in self.last.values():
            self.wait("sp", t)


class Rot:
    def __init__(self, tiles):
        self.tiles = tiles
        self.i = 0

    def next(self):
        t = self.tiles[self.i % len(self.tiles)]
        self.i += 1
        return t


def bc(ap, shape):
    return ap.to_broadcast(list(shape))


def build_program(debug=None):
    nc = bass.Bass("TRN2", target_bir_lowering=False)
    kb = KB(nc)
    P = 128

    def din(name, shape, dt=F32):
        return nc.dram_tensor(name, list(shape), dt, kind="ExternalInput").ap()

    x_in = din("x", [NSEQ, T, D])
    mem_in = din("mem", [NSEQ, MEMT, D])
    y_out = nc.dram_tensor("y", [NSEQ, T, D], F32, kind="ExternalOutput").ap()
    wnames = {
        "ffn1_w_gate": (D, DFF), "ffn1_w_up": (D, DFF), "ffn1_w_down": (DFF, D),
        "w_in": (D, INW), "w_mem_kv": (D, 2048), "w_br_na": (1024, D), "w_br_ssd": (2048, D),
        "w_br_mem": (1024, D), "w_out": (D, D),
        "ffn2_w_gate": (D, DFF), "ffn2_w_up": (D, DFF), "ffn2_w_down": (DFF, D),
    }
    wf = {n: din(n, s) for n, s in wnames.items()}
    wb = {n: nc.dram_tensor(n + "_bf", list(s), BF16).ap() for n, s in wnames.items()}
    wbt = {n: Tl() for n in wnames}
    gains_in = din("gains", [P, 64])
    ident_in = din("ident", [P, P])
    ssdc_in = din("ssdc", [P, 352])
    tri_in = din("tri", [P, 512])
    nab_in = din("nab", [16 * 128, NA_NSLOT * 128])
    nab_bf = nc.dram_tensor("nab_bf", [16 * 128, NA_NSLOT * 128], BF16).ap()
    x1s = nc.dram_tensor("x1s", [T, D], F32).ap()
    x1s_t = [Tl() for _ in range(4)]

    off = 0

    def alloc(n, dt):
        nonlocal off
        v = kb.view(off, n, dt)
        off += n * mybir.dt.size(dt)
        off = (off + 31) // 32 * 32
        return v

    gains = alloc(64, F32); gains_t = Tl()
    identb = alloc(128, BF16); identb_t = Tl()
    identf = alloc(128, F32); identf_t = Tl()
    UT_OFF = off
    uT = alloc(8 * T, BF16).rearrange("p (c t) -> p c t", c=8)
    uT_t = [Tl() for _ in range(4)]
    MG_OFF = off
    mg = alloc(8 * T, BF16).rearrange("p (c t) -> p c t", c=8)
    mg_t = [Tl() for _ in range(4)]
    PH0 = off

    G_FFN1, G_MIX, G_FFN2, G_MEMN, G_NAQ, G_NAK, G_MEMQ, G_MEMK, G_SSDN = 0, 8, 16, 24, 32, 33, 34, 36, 38

    kb.dma(gains, gains_in, writes=[gains_t])
    kb.dma(identf, ident_in, writes=[identf_t])
    kb.dma(identb, ident_in, writes=[identb_t], q="pool")
    wtoks = {}
    FAST = ("ffn1_w_gate", "ffn1_w_up", "ffn1_w_down")
    stg_rot = Rot([(kb.view(PH0 + k * 11264, 2816, F32), Tl()) for k in range(2)])
    stb_rot = Rot([(kb.view(PH0 + 22528 + k * 5632, 2816, BF16), Tl()) for k in range(2)])
    ci_ = 0
    for n in FAST:
        r, c = wnames[n]
        wtoks[n] = []
        rows_per = 2816 // c
        nblk = r // 128
        for b0 in range(0, nblk, rows_per):
            nb_ = min(rows_per, nblk - b0)
            (stg, stg_t) = stg_rot.next()
            (stb, stb_t) = stb_rot.next()
            src = wf[n][b0 * 128:(b0 + nb_) * 128, :].rearrange("(k p) n -> p k n", p=128)
            dst = wb[n][b0 * 128:(b0 + nb_) * 128, :].rearrange("(k p) n -> p k n", p=128)
            kb.dma(stg[:, 0:nb_ * c].rearrange("p (k n) -> p k n", k=nb_), src, writes=[stg_t])
            if ci_ % 2 == 0:
                kb.op("dve", lambda e: e.tensor_copy(out=stb[:, 0:nb_ * c], in_=stg[:, 0:nb_ * c]), reads=[stg_t], writes=[stb_t])
            else:
                kb.op("act", lambda e: e.activation(out=stb[:, 0:nb_ * c], in_=stg[:, 0:nb_ * c], func=AF.Copy), reads=[stg_t], writes=[stb_t])
            ci_ += 1
            wtoks[n].append(kb.dma(dst, stb[:, 0:nb_ * c].rearrange("p (k n) -> p k n", k=nb_), reads=[stb_t]))
    for n_ in FAST:
        for t_ in wtoks[n_]:
            kb.wait("pool", t_)
    for n, (r, c) in wnames.items():
        if n in FAST:
            continue
        nblk = r // 128
        step = 2 if c > 4096 else 8
        wtoks[n] = []
        for b0 in range(0, nblk, step):
            b1 = min(nblk, b0 + step)
            wtoks[n].append(kb.dma(wb[n][b0 * 128:b1 * 128, :], wf[n][b0 * 128:b1 * 128, :], writes=[], q="pool"))
    wtoks["nab"] = []
    for b0 in range(0, 16, 4):
        wtoks["nab"].append(kb.dma(nab_bf[b0 * 128:(b0 + 4) * 128, :], nab_in[b0 * 128:(b0 + 4) * 128, :], q="pool"))
    for t in (identb_t.w, gains_t.w, identf_t.w):
        kb.wait("sp", t)
    wready = set()

    def need(n):
        if n in wready:
            return
        wready.add(n)
        for t in wtoks[n]:
            kb.wait("sp", t)

    def rmsnorm_to_T(xt, xt_t, ntok_chunks, gcol, outT, outT_t, col0, scr):
        ss, ss_t, junk, junk_t, xn_rot, pst_rot = scr
        J = ntok_chunks
        kb.op("dve", lambda e: e.memset(ss[:, 0:J], 0.0), writes=[ss_t])
        for j in range(J):
            kb.op("act", lambda e: e.activation(out=junk, in_=xt[:, j, :], func=AF.Square, accum_out=ss[:, j:j + 1]),
                  reads=[xt_t], writes=[junk_t, ss_t])
        kb.op("act", lambda e: e.activation(out=ss[:, 0:J], in_=ss[:, 0:J], func=AF.Sqrt, bias=EPS, scale=1.0 / D),
              writes=[ss_t])
        kb.op("dve", lambda e: e.reciprocal(ss[:, 0:J], ss[:, 0:J]), writes=[ss_t])
        for j in range(J):
            xn, xn_t = xn_rot.next()
            kb.op("dve", lambda e: e.tensor_scalar(out=xn, in0=xt[:, j, :], scalar1=ss[:, j:j + 1], scalar2=None,
                                                    op0=ALU.mult), reads=[xt_t, ss_t], writes=[xn_t])
            pb, pb_t = pst_rot.next()
            kb.deps("pe", [xn_t, identb_t], [pb_t])
            inst = None
            for c in range(8):
                inst = kb.tr_raw(pb[:, c * 128:(c + 1) * 128], xn[:, c * 128:(c + 1) * 128], identb)
            kb.mm_done(inst, [xn_t, identb_t], [pb_t])
            g = gains[:, gcol:gcol + 8]
            kb.op("dve", lambda e: e.tensor_tensor(out=outT[:, :, col0 + j * 128: col0 + (j + 1) * 128],
                                                   in0=pb.rearrange("p (c t) -> p c t", c=8),
                                                   in1=bc(g.unsqueeze(2), [P, 8, 128]), op=ALU.mult),
                  reads=[pb_t, gains_t], writes=[outT_t])

    def norm_front(xt, xt_t, L, slot="XN"):
        ss, ss_t, junk, junk_t, _, _ = L["nscr"]
        XN, XN_t = L[slot]
        kb.op("dve", lambda e: e.memset(ss[:, 0:4], 0.0), writes=[ss_t])
        for j in range(4):
            kb.op("act", lambda e: e.activation(out=junk, in_=xt[:, j, :], func=AF.Square, accum_out=ss[:, j:j + 1]),
                  reads=[xt_t], writes=[junk_t, ss_t])
        kb.op("act", lambda e: e.activation(out=ss[:, 0:4], in_=ss[:, 0:4], func=AF.Sqrt, bias=EPS, scale=1.0 / D),
              writes=[ss_t])
        kb.op("dve", lambda e: e.reciprocal(ss[:, 0:4], ss[:, 0:4]), writes=[ss_t])
        for j in range(4):
            kb.op("dve", lambda e: e.tensor_scalar(out=XN[:, j, :], in0=xt[:, j, :], scalar1=ss[:, j:j + 1], scalar2=None,
                                                   op0=ALU.mult), reads=[xt_t, ss_t], writes=[XN_t])

    def norm_back(gcol, L, slot="XN", out=None, col0=0):
        _, _, _, _, _, pst_rot = L["nscr"]
        XN, XN_t = L[slot]
        uf, uf_t = L["uf"] if out is None else out
        g = gains[:, gcol:gcol + 8]
        for j in range(4):
            pb, pb_t = pst_rot.next()
            kb.deps("pe", [XN_t, identb_t], [pb_t])
            inst = None
            for c in range(8):
                inst = kb.tr_raw(pb[:, c * 128:(c + 1) * 128], XN[:, j, c * 128:(c + 1) * 128], identb)
            kb.mm_done(inst, [XN_t, identb_t], [pb_t])
            kb.op("dve", lambda e: e.tensor_tensor(out=uf[:, :, col0 + j * 128:col0 + (j + 1) * 128],
                                                   in0=pb.rearrange("p (c t) -> p c t", c=8),
                                                   in1=bc(g.unsqueeze(2), [P, 8, 128]), op=ALU.mult),
                  reads=[pb_t, gains_t], writes=[uf_t])

    def gate_up(wg, wu, L):
        uf, uf_t = L["uf"]
        h, h_t = L["h"]
        wrot = L["wrot"]
        HB = 2
        nblk = NHC // HB

        def load(bi):
            need(wg); need(wu)
            (wgb, wgb_t), (wub, wub_t) = wrot.next()
            src_g = wb[wg].rearrange("(c p) n -> p c n", p=128)[:, :, bi * HB * 128:(bi + 1) * HB * 128]
            src_u = wb[wu].rearrange("(c p) n -> p c n", p=128)[:, :, bi * HB * 128:(bi + 1) * HB * 128]
            kb.dma(wgb, src_g, writes=[wgb_t])
            kb.dma(wub, src_u, writes=[wub_t])
            return (wgb, wgb_t), (wub, wub_t)

        nxt = load(0)
        for bi in range(nblk):
            (wgb, wgb_t), (wub, wub_t) = nxt
            if bi + 1 < nblk:
                nxt = load(bi + 1)
            for hl in range(HB):
                hc = bi * HB + hl
                pg, pg_t = L["pgu"].next()
                pu, pu_t = L["pgu"].next()
                kb.mm(pg_t, pg, [(wgb[:, c, hl * 128:(hl + 1) * 128], uf[:, c, :]) for c in range(8)], [wgb_t, uf_t])
                kb.mm(pu_t, pu, [(wub[:, c, hl * 128:(hl + 1) * 128], uf[:, c, :]) for c in range(8)], [wub_t, uf_t])
                sg, sg_t = L["sg"].next()
                kb.op("act", lambda e: e.activation(out=sg, in_=pg, func=AF.Silu), reads=[pg_t], writes=[sg_t])
                kb.op("dve", lambda e: e.tensor_tensor(out=h[:, hc, :], in0=pu, in1=sg, op=ALU.mult),
                      reads=[pu_t, sg_t], writes=[h_t])

    def down(xt, xt_t, xo, xo_t, wd_sb, wd_t, L):
        h, h_t = L["h"]
        for j in range(4):
            for hf in range(2):
                pd, pd_t = L["pd"].next()
                kb.mm(pd_t, pd, [(h[:, hc, j * 128:(j + 1) * 128], wd_sb[:, hc, hf * 512:(hf + 1) * 512])
                                 for hc in range(NHC)], [h_t, wd_t])
                if xo_t is xt_t:
                    kb.op("dve", lambda e: e.scalar_tensor_tensor(out=xo[:, j, hf * 512:(hf + 1) * 512], in0=pd, scalar=0.5,
                                                                  in1=xt[:, j, hf * 512:(hf + 1) * 512],
                                                                  op0=ALU.mult, op1=ALU.add),
                          reads=[pd_t], writes=[xo_t])
                else:
                    kb.op("dve", lambda e: e.scalar_tensor_tensor(out=xo[:, j, hf * 512:(hf + 1) * 512], in0=pd, scalar=0.5,
                                                                  in1=xt[:, j, hf * 512:(hf + 1) * 512],
                                                                  op0=ALU.mult, op1=ALU.add),
                          reads=[pd_t, xt_t], writes=[xo_t])

    def ffn_layout(base, hole, with_xo=True):
        nonlocal off
        L = {}
        off = hole
        L["xt"] = Rot([(alloc(4 * D, F32).rearrange("p (j d) -> p j d", j=4), Tl()) for _ in range(2)])
        off = base
        L["wd"] = (alloc(NHC * D, BF16).rearrange("p (h n) -> p h n", h=NHC), Tl())
        L["wrot"] = Rot([((alloc(8 * 256, BF16).rearrange("p (c n) -> p c n", c=8), Tl()),
                          (alloc(8 * 256, BF16).rearrange("p (c n) -> p c n", c=8), Tl())) for _ in range(2)])
        if with_xo:
            L["xo"] = Rot([(alloc(4 * D, F32).rearrange("p (j d) -> p j d", j=4), Tl()) for _ in range(1)])
        L["uf"] = (alloc(8 * 512, BF16).rearrange("p (c t) -> p c t", c=8), Tl())
        L["XN"] = (alloc(4 * D, BF16).rearrange("p (j d) -> p j d", j=4), Tl())
        if with_xo:
            L["XN2"] = (alloc(4 * D, BF16).rearrange("p (j d) -> p j d", j=4), Tl())
        L["h"] = (alloc(NHC * 512, BF16).rearrange("p (h t) -> p h t", h=NHC), Tl())
        L["sg"] = Rot([(alloc(512, F32), Tl()) for _ in range(2)])
        ss = (alloc(8, F32), Tl())
        junk = (alloc(D, F32), Tl())
        xn_rot = Rot([(alloc(D, BF16), Tl()) for _ in range(2)])
        pst_rot = Rot([(kb.bank_bf(0), Tl(psum=True)), (kb.bank_bf(1), Tl(psum=True))])
        L["nscr"] = (ss[0], ss[1], junk[0], junk[1], xn_rot, pst_rot)
        L["pgu"] = Rot([(kb.bank(2), Tl(psum=True)), (kb.bank(3), Tl(psum=True)), (kb.bank(4), Tl(psum=True)), (kb.bank(5), Tl(psum=True))])
        L["pd"] = Rot([(kb.bank(6), Tl(psum=True)), (kb.bank(7), Tl(psum=True))])
        return L

    def tok_major(dram2d, t):
        return dram2d[t * 512:(t + 1) * 512, :].rearrange("(j p) d -> p j d", p=128)

    def phase_A(s):
        L = ffn_layout(PH0, MG_OFF)
        wd_sb, wd_t = L["wd"]
        need("ffn1_w_down")
        kb.dma(wd_sb, wb["ffn1_w_down"].rearrange("(h p) n -> p h n", p=128), writes=[wd_t])
        tiles = []
        for t in range(2):
            xt, xt_t = L["xt"].next()
            kb.dma(xt, tok_major(x_in[s], t), writes=[xt_t])
            tiles.append((xt, xt_t))
        norm_front(tiles[0][0], tiles[0][1], L)
        for t in range(4):
            xt, xt_t = tiles[t]
            norm_back(G_FFN1, L)
            gate_up("ffn1_w_gate", "ffn1_w_up", L)
            if t >= 1:
                norm_back(G_MIX, L, slot="XN2", out=(uT, uT_t[t - 1]), col0=(t - 1) * 512)
            if t + 1 < 4:
                norm_front(tiles[t + 1][0], tiles[t + 1][1], L)
            xo, xo_t = L["xo"].next()
            down(xt, xt_t, xo, xo_t, wd_sb, wd_t, L)
            kb.dma(tok_major(x1s, t), xo, reads=[xo_t], writes=[x1s_t[t]])
            if t + 2 < 4:
                xn_, xn_t_ = L["xt"].next()
                kb.dma(xn_, tok_major(x_in[s], t + 2), writes=[xn_t_])
                tiles.append((xn_, xn_t_))
            norm_front(xo, xo_t, L, slot="XN2")
        norm_back(G_MIX, L, slot="XN2", out=(uT, uT_t[3]), col0=3 * 512)

    def phase_C(s):
        L = ffn_layout(PH0, UT_OFF, with_xo=False)
        wd_sb, wd_t = L["wd"]
        need("ffn2_w_down"); need("w_out")
        kb.dma(wd_sb, wb["ffn2_w_down"].rearrange("(h p) n -> p h n", p=128), writes=[wd_t])
        wo = alloc(8 * D, BF16).rearrange("p (c n) -> p c n", c=8)
        wo_t = Tl()
        kb.dma(wo, wb["w_out"].rearrange("(c p) n -> p c n", p=128), writes=[wo_t])

        def front(t, xt, xt_t):
            for j in range(4):
                for hf in range(2):
                    pd, pd_t = L["pd"].next()
                    kb.mm(pd_t, pd, [(mg[:, c, t * 512 + j * 128: t * 512 + (j + 1) * 128], wo[:, c, hf * 512:(hf + 1) * 512])
                                     for c in range(8)], [mg_t[t], wo_t])
                    kb.op("dve", lambda e: e.tensor_tensor(out=xt[:, j, hf * 512:(hf + 1) * 512], in0=pd,
                                                           in1=xt[:, j, hf * 512:(hf + 1) * 512], op=ALU.add),
                          reads=[pd_t], writes=[xt_t])
            norm_front(xt, xt_t, L)

        tiles = []
        for t in range(2):
            xt, xt_t = L["xt"].next()
            kb.dma(xt, tok_major(x1s, t), reads=[x1s_t[t]], writes=[xt_t])
            tiles.append((xt, xt_t))
        front(0, tiles[0][0], tiles[0][1])
        for t in range(4):
            xt, xt_t = tiles[t]
            norm_back(G_FFN2, L)
            gate_up("ffn2_w_gate", "ffn2_w_up", L)
            if t + 1 < 4:
                front(t + 1, tiles[t + 1][0], tiles[t + 1][1])
            down(xt, xt_t, xt, xt_t, wd_sb, wd_t, L)
            kb.dma(tok_major(y_out[s], t), xt, reads=[xt_t])
            if t + 2 < 4:
                xn_, xn_t_ = L["xt"].next()
                kb.dma(xn_, tok_major(x1s, t + 2), reads=[x1s_t[t + 2]], writes=[xn_t_])
                tiles.append((xn_, xn_t_))

    BR = debug.get("branches", ("na", "ssd", "mem")) if debug else ("na", "ssd", "mem")
    GCOL = {"na": G0, "ssd": G0 + 1024, "mem": G0 + 2048}
    win_v = wb["w_in"].rearrange("(c p) n -> p c n", p=128)

    gsc = nc.dram_tensor("gsc", [8, 128, T], F32).ap()
    gsc_t = [[Tl() for _ in range(4)] for _ in range(8)]

    def merge_branch(name, wname, krow0, oT, oT_t, first, L, nk=8, gate="compute"):
        need(wname); need("w_in")
        wv = wb[wname].rearrange("(k p) n -> p k n", p=128)
        k0 = krow0 // 128
        def wload(f):
            (wbr, wbr_t), (wgt, wgt_t) = L["mw"].next()
            kb.dma(wbr[:, 0:nk, :], wv[:, k0:k0 + nk, f * 128:(f + 1) * 128], writes=[wbr_t])
            if gate != "load":
                kb.dma(wgt, win_v[:, :, GCOL[name] + f * 128: GCOL[name] + (f + 1) * 128], writes=[wgt_t])
            return (wbr, wbr_t), (wgt, wgt_t)

        nxt_w = wload(0)
        for f in range(8):
            (wbr, wbr_t), (wgt, wgt_t) = nxt_w
            if f + 1 < 8:
                nxt_w = wload(f + 1)
            for t in range(4):
                pbr, pbr_t = L["mp"].next()
                sg, sg_t = L["msg"].next()
                if gate == "load":
                    kb.dma(sg, gsc[f, :, t * 512:(t + 1) * 512], reads=[gsc_t[f][t]], writes=[sg_t])
                else:
                    pgl, pgl_t = L["mp"].next()
                    kb.mm(pgl_t, pgl, [(wgt[:, c, :], uT[:, c, t * 512:(t + 1) * 512]) for c in range(8)], [wgt_t, uT_t[t]])
                kb.mm(pbr_t, pbr, [(wbr[:, k, :], oT[:, k, t * 512:(t + 1) * 512]) for k in range(nk)], [wbr_t, oT_t])
                if gate != "load":
                    kb.op("act", lambda e: e.activation(out=sg, in_=pgl, func=AF.Sigmoid), reads=[pgl_t], writes=[sg_t])
                if gate == "store":
                    kb.dma(gsc[f, :, t * 512:(t + 1) * 512], sg, reads=[sg_t], writes=[gsc_t[f][t]])
                if first:
                    kb.op("dve", lambda e: e.tensor_tensor(out=mg[:, f, t * 512:(t + 1) * 512], in0=pbr, in1=sg, op=ALU.mult),
                          reads=[pbr_t, sg_t], writes=[mg_t[t]])
                else:
                    kb.op("dve", lambda e: e.tensor_tensor(out=sg, in0=pbr, in1=sg, op=ALU.mult),
                          reads=[pbr_t], writes=[sg_t])
                    kb.op("dve", lambda e: e.tensor_tensor(out=mg[:, f, t * 512:(t + 1) * 512],
                                                            in0=mg[:, f, t * 512:(t + 1) * 512], in1=sg, op=ALU.add),
                          reads=[sg_t], writes=[mg_t[t]])

    def merge_layout():
        L = {}
        L["mw"] = Rot([((alloc(8 * 128, BF16).rearrange("p (c n) -> p c n", c=8), Tl()),
                        (alloc(8 * 128, BF16).rearrange("p (c n) -> p c n", c=8), Tl())) for _ in range(2)])
        L["msg"] = Rot([(alloc(512, F32), Tl()) for _ in range(2)])
        return L

    def mem_branch(s, first):
        nonlocal off
        off = PH0
        need("w_in"); need("w_mem_kv")
        oT = alloc(8 * T, BF16).rearrange("p (c t) -> p c t", c=8)
        oT_t = Tl()
        L = merge_layout()
        onesb = alloc(128, BF16); onesb_t = Tl()
        kb.op("pool", lambda e: e.memset(onesb, 1.0), writes=[onesb_t])
        memx = alloc(2 * D, F32).rearrange("p (j d) -> p j d", j=2); memx_t = Tl()
        mT = alloc(8 * 256, BF16).rearrange("p (c t) -> p c t", c=8); mT_t = Tl()
        ss = (alloc(8, F32), Tl()); junk = (alloc(D, F32), Tl())
        xn_rot = Rot([(alloc(D, BF16), Tl()) for _ in range(2)])
        bk = [(kb.bank(i), Tl(psum=True)) for i in range(8)]
        pst_rot = Rot([(kb.bank_bf(0), bk[0][1])])
        kTn = alloc(8 * 256, BF16).rearrange("p (c t) -> p c t", c=8); kTn_t = Tl()
        vm = alloc(2 * D, BF16).rearrange("p (j d) -> p j d", j=2); vm_t = Tl()
        wkv_rot = Rot([(alloc(8 * 512, BF16).rearrange("p (c n) -> p c n", c=8), Tl()) for _ in range(2)])
        wq = alloc(8 * D, BF16).rearrange("p (c n) -> p c n", c=8); wq_t = Tl()
        sq_rot = Rot([(alloc(512, BF16), Tl()) for _ in range(2)])
        rs = alloc(512, F32); rs_t = Tl()
        qn = alloc(2 * 512, BF16).rearrange("p (j d) -> p j d", j=2); qn_t = Tl()
        PT = alloc(2 * 512, BF16).rearrange("p (j d) -> p j d", j=2); PT_t = Tl()
        rden = alloc(512, F32); rden_t = Tl()

        kb.dma(memx, mem_in[s].rearrange("(j p) d -> p j d", p=128), writes=[memx_t])
        kb.dma(wq, win_v[:, :, QM0:QM0 + 1024], writes=[wq_t])
        rmsnorm_to_T(memx, memx_t, 2, G_MEMN, mT, mT_t, 0, (ss[0], ss[1], junk[0], junk[1], xn_rot, pst_rot))
        wkv_v = wb["w_mem_kv"].rearrange("(c p) n -> p c n", p=128)
        for blk in range(2):
            wk, wk_t = wkv_rot.next()
            kb.dma(wk, wkv_v[:, :, blk * 512:(blk + 1) * 512], writes=[wk_t])
            for hl in range(2):
                h = blk * 2 + hl
                pk, pk_t = bk[1]
                kb.deps("pe", [wk_t, mT_t], [pk_t])
                inst = None
                for dc in range(2):
                    for c in range(8):
                        inst = kb.mm_raw(pk[:, dc * 256:(dc + 1) * 256], wk[:, c, hl * 256 + dc * 128: hl * 256 + (dc + 1) * 128],
                                         mT[:, c, :], c == 0, c == 7)
                kb.mm_done(inst, [wk_t, mT_t], [pk_t])
                sq, sq_t = sq_rot.next()
                kb.op("act", lambda e: e.activation(out=sq, in_=pk, func=AF.Square), reads=[pk_t], writes=[sq_t])
                pss, pss_t = bk[2]
                kb.mm(pss_t, pss[:, 0:256], [(onesb, sq[:, 0:256]), (onesb, sq[:, 256:512])], [onesb_t, sq_t])
                kb.op("act", lambda e: e.activation(out=rs[:, 0:256], in_=pss[:, 0:256], func=AF.Ln, bias=EPS, scale=1.0 / 256),
                      reads=[pss_t], writes=[rs_t])
                kb.op("act", lambda e: e.activation(out=rs[:, 0:256], in_=rs[:, 0:256], func=AF.Exp, scale=-0.5), writes=[rs_t])
                for dc in range(2):
                    kb.op("dve", lambda e: e.scalar_tensor_tensor(out=kTn[:, h * 2 + dc, :], in0=pk[:, dc * 256:(dc + 1) * 256],
                                                                  scalar=gains[:, G_MEMK + dc:G_MEMK + dc + 1], in1=rs[:, 0:256],
                                                                  op0=ALU.mult, op1=ALU.mult),
                          reads=[pk_t, rs_t, gains_t], writes=[kTn_t])
        for blk in range(2):
            wk, wk_t = wkv_rot.next()
            kb.dma(wk, wkv_v[:, :, 1024 + blk * 512: 1024 + (blk + 1) * 512], writes=[wk_t])
            for mc in range(2):
                pv, pv_t = bk[3 + mc]
                kb.mm(pv_t, pv, [(mT[:, c, mc * 128:(mc + 1) * 128], wk[:, c, :]) for c in range(8)], [mT_t, wk_t])
                kb.op("act", lambda e: e.activation(out=vm[:, mc, blk * 512:(blk + 1) * 512], in_=pv, func=AF.Copy),
                      reads=[pv_t], writes=[vm_t])
        its = [(t, h) for t in range(4) for h in range(4)]
        qn_rot = Rot([(qn, qn_t), (alloc(2 * 512, BF16).rearrange("p (j d) -> p j d", j=2), Tl())])
        PT_rot2 = Rot([(PT, PT_t), (alloc(2 * 512, BF16).rearrange("p (j d) -> p j d", j=2), Tl())])
        st = {}

        def s1a(k):
            t, h = its[k]
            tsl = slice(t * 512, (t + 1) * 512)
            pq = [bk[0], bk[1]]
            for dc in range(2):
                kb.mm(pq[dc][1], pq[dc][0], [(wq[:, c, h * 256 + dc * 128: h * 256 + (dc + 1) * 128], uT[:, c, tsl])
                                             for c in range(8)], [wq_t, uT_t[t]])
            sqs = []
            for dc in range(2):
                sq, sq_t = sq_rot.next()
                kb.op("act", lambda e: e.activation(out=sq, in_=pq[dc][0], func=AF.Square), reads=[pq[dc][1]], writes=[sq_t])
                sqs.append((sq, sq_t))
            st[k] = sqs

        def s1b(k):
            sqs = st[k]
            pq = [bk[0], bk[1]]
            pss, pss_t = bk[2]
            kb.mm(pss_t, pss, [(onesb, sqs[0][0]), (onesb, sqs[1][0])], [onesb_t, sqs[0][1], sqs[1][1]])
            kb.op("act", lambda e: e.activation(out=rs, in_=pss, func=AF.Ln, bias=EPS, scale=1.0 / 256),
                  reads=[pss_t], writes=[rs_t])
            kb.op("act", lambda e: e.activation(out=rs, in_=rs, func=AF.Exp, scale=-0.5), writes=[rs_t])
            qn_, qn_t_ = qn_rot.next()
            for dc in range(2):
                kb.op("dve", lambda e: e.scalar_tensor_tensor(out=qn_[:, dc, :], in0=pq[dc][0],
                                                              scalar=gains[:, G_MEMQ + dc:G_MEMQ + dc + 1], in1=rs,
                                                              op0=ALU.mult, op1=ALU.mult),
                      reads=[pq[dc][1], rs_t, gains_t], writes=[qn_t_])
            st[k] = (qn_, qn_t_)

        def s2a(k):
            t, h = its[k]
            qn_, qn_t_ = st[k]
            PT_, PT_t_ = PT_rot2.next()
            for mc in range(2):
                pS, pS_t = bk[3 + mc]
                kb.mm(pS_t, pS, [(kTn[:, h * 2 + dc, mc * 128:(mc + 1) * 128], qn_[:, dc, :]) for dc in range(2)], [kTn_t, qn_t_])
                kb.op("act", lambda e: e.activation(out=PT_[:, mc, :], in_=pS, func=AF.Exp, scale=1.0 / 16.0),
                      reads=[pS_t], writes=[PT_t_])
            st[k] = (PT_, PT_t_)

        def s2b(k):
            t, h = its[k]
            tsl = slice(t * 512, (t + 1) * 512)
            PT_, PT_t_ = st.pop(k)
            pden, pden_t = bk[7]
            kb.mm(pden_t, pden, [(onesb, PT_[:, mc, :]) for mc in range(2)], [onesb_t, PT_t_])
            kb.op("dve", lambda e: e.reciprocal(rden, pden), reads=[pden_t], writes=[rden_t])
            for dc in range(2):
                po, po_t = bk[5 + dc]
                kb.mm(po_t, po, [(vm[:, mc, h * 256 + dc * 128: h * 256 + (dc + 1) * 128], PT_[:, mc, :]) for mc in range(2)],
                      [vm_t, PT_t_])
                kb.op("dve", lambda e: e.tensor_tensor(out=oT[:, h * 2 + dc, tsl], in0=po, in1=rden, op=ALU.mult),
                      reads=[po_t, rden_t], writes=[oT_t])

        for i in range(len(its) + 1):
            if i < len(its):
                s1a(i)
            if i >= 1:
                s2a(i - 1)
            if i < len(its):
                s1b(i)
            if i >= 1:
                s2b(i - 1)
        L["mp"] = Rot([bk[0], bk[1], bk[2], bk[3]])
        merge_branch("mem", "w_br_mem", 0, oT, oT_t, first, L)

    def na_pat(i):
        if i == 0:
            return (0, 4, 0)
        if i == 1:
            return (4, 4, 0)
        if i == 14:
            return (13, 4, 12)
        if i == 15:
            return (17, 4, 12)
        return (8, 5, i - 2)

    def na_branch(s, first):
        nonlocal off
        off = PH0
        need("w_in"); need("nab")
        oT = alloc(8 * T, BF16).rearrange("p (c t) -> p c t", c=8)
        oT_t = Tl()
        L = merge_layout()
        blk1 = alloc(128, BF16); blk1_t = Tl()
        kb.op("pool", lambda e: e.memset(blk1, 0.0), writes=[blk1_t])
        kb.op("pool", lambda e: e.memset(blk1[0:64, 0:64], 1.0), writes=[blk1_t])
        kb.op("pool", lambda e: e.memset(blk1[64:128, 64:128], 1.0), writes=[blk1_t])
        w_rot = Rot([tuple((alloc(8 * 128, BF16).rearrange("p (c n) -> p c n", c=8), Tl()) for _ in range(3)) for _ in range(2)])
        qk_rot = Rot([((alloc(T, BF16), Tl()), (alloc(T, BF16), Tl())) for _ in range(2)])
        va_rot = []
        for _ in range(2):
            va = alloc(16 * 2 * 128, BF16).rearrange("p (j h d) -> p j h d", j=16, h=2)
            va_t = Tl()
            kb.op("pool", lambda e: e.memset(va[:, :, :, 64:128], 1.0), writes=[va_t])
            va_rot.append((va, va_t))
        va_rot = Rot(va_rot)
        nb_rot = Rot([(alloc(NA_NSLOT * 128, BF16), Tl()) for _ in range(2)])
        PT_rot = Rot([(alloc(768, BF16), Tl()) for _ in range(3)])
        sq_rot = Rot([(alloc(512, BF16), Tl()) for _ in range(2)])
        rs_rot = Rot([(alloc(512, F32), Tl()) for _ in range(2)])
        rden_rot = Rot([(alloc(128, F32), Tl()) for _ in range(2)])
        pS_rot = Rot([(kb.bank(0, 2), Tl(psum=True)), (kb.bank(2, 2), Tl(psum=True))])
        ring = [(kb.bank(4 + b), Tl(psum=True)) for b in range(4)]
        po_rot = Rot([ring[0], ring[1]])
        pq_rot = Rot([ring[2], (kb.bank(0), pS_rot.tiles[0][1]), (kb.bank(2), pS_rot.tiles[1][1])])
        pss_b = ring[3]
        pss_rot = Rot([pss_b, po_rot.tiles[0], po_rot.tiles[1]])
        pv_rot = Rot([(kb.bank(6), pq_rot.tiles[0][1]), (kb.bank(7), pss_b[1])])

        for hp in range(8):
            (wq, wq_t), (wk, wk_t), (wv, wv_t) = w_rot.next()
            kb.dma(wq, win_v[:, :, Q0 + hp * 128: Q0 + (hp + 1) * 128], writes=[wq_t])
            kb.dma(wk, win_v[:, :, K0 + hp * 128: K0 + (hp + 1) * 128], writes=[wk_t])
            kb.dma(wv, win_v[:, :, V0 + hp * 128: V0 + (hp + 1) * 128], writes=[wv_t])
            (qT, qT_t), (kT, kT_t) = qk_rot.next()
            jobs = [(w_, w_t_, dst_, dst_t_, gcol_, sc_, bs_, t_)
                    for (w_, w_t_, dst_, dst_t_, gcol_, sc_, bs_) in ((wq, wq_t, qT, qT_t, G_NAQ, 1.0, 64 * EPS), (wk, wk_t, kT, kT_t, G_NAK, 1.0 / 64, EPS))
                    for t_ in range(4)]

            def proj(job):
                w_, w_t_, dst_, dst_t_, gcol_, sc_, bs_, t_ = job
                pq, pq_t = pq_rot.next()
                kb.mm(pq_t, pq, [(w_[:, c, :], uT[:, c, t_ * 512:(t_ + 1) * 512]) for c in range(8)], [w_t_, uT_t[t_]])
                return pq, pq_t

            cur = proj(jobs[0])
            for ji, job in enumerate(jobs):
                w_, w_t_, dst_, dst_t_, gcol_, sc_, bs_, t_ = job
                pq, pq_t = cur
                if ji + 1 < len(jobs):
                    cur = proj(jobs[ji + 1])
                tsl = slice(t_ * 512, (t_ + 1) * 512)
                sq, sq_t = sq_rot.next()
                kb.op("act", lambda e: e.activation(out=sq, in_=pq, func=AF.Square), reads=[pq_t], writes=[sq_t])
                pss, pss_t = pss_rot.next()
                kb.mm(pss_t, pss, [(blk1, sq)], [blk1_t, sq_t])
                rs, rs_t = rs_rot.next()
                kb.op("act", lambda e: e.activation(out=rs, in_=pss, func=AF.Sqrt, bias=bs_, scale=sc_),
                      reads=[pss_t], writes=[rs_t])
                kb.op("dve", lambda e: e.reciprocal(rs, rs), writes=[rs_t])
                kb.op("dve", lambda e: e.scalar_tensor_tensor(out=dst_[:, tsl], in0=pq, scalar=gains[:, gcol_:gcol_ + 1], in1=rs,
                                                              op0=ALU.mult, op1=ALU.mult),
                      reads=[pq_t, rs_t, gains_t], writes=[dst_t_])
            va, va_t = va_rot.next()
            for j0 in range(0, 16, 4):
                pv, pv_t = pv_rot.next()
                kb.deps("pe", [wv_t, uT_t[j0 // 4]], [pv_t])
                inst = None
                for jj in range(4):
                    j = j0 + jj
                    for c in range(8):
                        inst = kb.mm_raw(pv[:, jj * 128:(jj + 1) * 128], uT[:, c, j * 128:(j + 1) * 128], wv[:, c, :], c == 0, c == 7)
                kb.mm_done(inst, [wv_t, uT_t[j0 // 4]], [pv_t])
                kb.op("act", lambda e: e.activation(out=va[:, j0:j0 + 4, :, 0:64],
                                                    in_=pv.rearrange("p (j h d) -> p j h d", j=4, h=2), func=AF.Copy),
                      reads=[pv_t], writes=[va_t])
            for hl in range(2):
                h = hp * 2 + hl
                hb = hl * 64
                nbt, nbt_t = nb_rot.next()
                kb.dma(nbt, nab_bf[h * 128:(h + 1) * 128, :], writes=[nbt_t])

                def emit_S(j):
                    ilo, ihi = _na_blocks(j)
                    nq = ihi - ilo + 1
                    slot0 = _NA_SLOT[j]
                    pS, pS_t = pS_rot.next()
                    kb.deps("pe", [kT_t, qT_t, nbt_t, identb_t], [pS_t])
                    inst = None
                    for (c0, c1) in ((0, min(nq, 4)), (4, nq)):
                        if c1 <= c0:
                            continue
                        kb.mm_raw(pS[:, c0 * 128:c1 * 128], kT[hb:hb + 64, j * 128:(j + 1) * 128],
                                  qT[hb:hb + 64, (ilo + c0) * 128:(ilo + c1) * 128], True, False)
                        inst = kb.mm_raw(pS[:, c0 * 128:c1 * 128], identb, nbt[:, (slot0 + c0) * 128:(slot0 + c1) * 128], False, True)
                    kb.mm_done(inst, [kT_t, qT_t, nbt_t, identb_t], [pS_t])
                    PT, PT_t = PT_rot.next()
                    kb.op("act", lambda e: e.activation(out=PT[:, 0:nq * 128], in_=pS[:, 0:nq * 128], func=AF.Exp),
                          reads=[pS_t], writes=[PT_t])
                    return PT, PT_t

                def finish_block(i):
                    rb, rb_t = ring[(i % 8) // 2]
                    col = (i % 2) * 128
                    rden, rden_t = rden_rot.next()
                    kb.op("dve", lambda e: e.reciprocal(rden[0:64, :], rb[64:128, col:col + 128]), reads=[rb_t], writes=[rden_t])
                    kb.op("dve", lambda e: e.tensor_tensor(out=oT[hb:hb + 64, hp, i * 128:(i + 1) * 128], in0=rb[0:64, col:col + 128],
                                                           in1=rden[0:64, :], op=ALU.mult),
                          reads=[rb_t, rden_t], writes=[oT_t])

                cur = emit_S(0)
                done_prev = []
                for j in range(16):
                    nxt = emit_S(j + 1) if j + 1 < 16 else None
                    PT, PT_t = cur
                    ilo, ihi = _na_blocks(j)
                    groups = []
                    for i in range(ilo, ihi + 1):
                        lo, nk_ = _na_chunks(i)
                        key = ((i % 8) // 2, (j == lo) and (i % 2 == 0), j == lo + nk_ - 1)
                        if groups and groups[-1][0] == key and groups[-1][2] == i - 1:
                            groups[-1][2] = i
                        else:
                            groups.append([key, i, i])
                    banks_w = sorted(set(g_[0][0] for g_ in groups))
                    kb.deps("pe", [va_t, PT_t], [ring[b_][1] for b_ in banks_w])
                    inst = None
                    for (bank_, st_, sp_), i0, i1 in groups:
                        rb, rb_t = ring[bank_]
                        inst = kb.mm_raw(rb[:, (i0 % 2) * 128:(i1 % 2 + 1) * 128], va[:, j, hl, :],
                                         PT[:, (i0 - ilo) * 128:(i1 - ilo + 1) * 128], st_, sp_, skip=True)
                    kb.mm_done(inst, [va_t, PT_t], [ring[b_][1] for b_ in banks_w])
                    for i in done_prev:
                        finish_block(i)
                    done_prev = [i for i in range(ilo, ihi + 1) if j == _na_chunks(i)[0] + _na_chunks(i)[1] - 1]
                    cur = nxt
                for i in done_prev:
                    finish_block(i)
        L["mp"] = Rot([(kb.bank(0), pS_rot.tiles[0][1]), (kb.bank(2), pS_rot.tiles[1][1]), po_rot.tiles[0], po_rot.tiles[1]])
        merge_branch("na", "w_br_na", 0, oT, oT_t, first, L)

    def ssd_branch(s, first):
        nonlocal off
        off = PH0
        need("w_in")
        oT = alloc(4 * T, BF16).rearrange("p (c t) -> p c t", c=4)
        oT_t = Tl()
        L = merge_layout()
        sc = alloc(352, F32); sc_t = Tl()
        kb.dma(sc, ssdc_in, writes=[sc_t])
        convw = sc[:, 0:160].rearrange("p (c k) -> p c k", k=5)
        convb = sc[:, 160:192]
        tri = alloc(256, F32); tri_t = Tl()
        kb.dma(tri, tri_in[:, 0:256], writes=[tri_t])
        triU = tri[:, 0:128]; triL = tri[:, 128:256]
        mk = alloc(1024, BF16).rearrange("p (d r l) -> p d r l", d=2, r=4); mk_t = Tl()
        for d in range(2):
            for r in range(4):
                kb.dma(mk[:, d, r, :], tri_in[:, 256 + d * 128: 256 + (d + 1) * 128], writes=[mk_t], q="pool")
        onesf = alloc(128, F32); onesf_t = Tl()
        kb.op("pool", lambda e: e.memset(onesf, 1.0), writes=[onesf_t])
        wdt = alloc(8 * 64, BF16).rearrange("p (c n) -> p c n", c=8); wdt_t = Tl()
        kb.dma(wdt, win_v[:, :, DT0:DT0 + 64], writes=[wdt_t])
        w_rot = Rot([(alloc(8 * 128, BF16).rearrange("p (c n) -> p c n", c=8), Tl()) for _ in range(2)])
        wz = alloc(8 * 256, BF16).rearrange("p (c n) -> p c n", c=8); wz_t = Tl()
        pre_l = []
        for _ in range(2):
            pre = alloc(2052, BF16); pre_t = Tl()
            kb.op("pool", lambda e: e.memset(pre[:, 0:2], 0.0), writes=[pre_t])
            kb.op("pool", lambda e: e.memset(pre[:, 2050:2052], 0.0), writes=[pre_t])
            pre_l.append((pre, pre_t))
        pre_rot = Rot(pre_l)
        dg_rot = Rot([(alloc(5 * 128, BF16).rearrange("p (k n) -> p k n", k=5), Tl()) for _ in range(2)])
        xfm_rot = Rot([(alloc(2048, BF16), Tl()) for _ in range(2)])
        Bfm = alloc(2048, BF16); Bfm_t = Tl()
        Cfm = alloc(2048, BF16); Cfm_t = Tl()
        xtm = alloc(16 * 256, BF16).rearrange("p (j d) -> p j d", j=16); xtm_t = Tl()
        Btm = alloc(16 * 128, BF16).rearrange("p (j d) -> p j d", j=16); Btm_t = Tl()

        def small():
            return alloc(128, F32).rearrange("p (j h) -> p j h", j=16), Tl()
        dt, dt_t = small(); lndt, lndt_t = small(); dA, dA_t = small(); Ac, Ac_t = small(); tot, tot_t = small()
        nb, nb_t = small(); wS, wS_t = small(); eA, eA_t = small(); cd, cd_t = small()
        a8 = alloc(8, F32); a8_t = Tl()
        b8 = alloc(8, F32); b8_t = Tl()
        d4 = alloc(4, F32); d4_t = Tl()
        identD = alloc(4 * 128, BF16).rearrange("p (r n) -> p r n", r=4); identD_t = Tl()
        prev = [(alloc(16 * 256, BF16).rearrange("p (j d) -> p j d", j=16), Tl()) for _ in range(2)]
        Hs = [(alloc(256, F32), Tl()) for _ in range(2)]
        xwas = [(alloc(16 * 256, BF16).rearrange("p (j d) -> p j d", j=16), Tl()) for _ in range(2)]
        Gt_rot = Rot([(alloc(128, F32), Tl()) for _ in range(2)])
        E_rot = Rot([((alloc(512, F32), Tl()), (alloc(512, F32), Tl())) for _ in range(2)])
        Mt_rot = Rot([(alloc(512, BF16).rearrange("p (r l) -> p r l", r=4), Tl()) for _ in range(2)])
        sz_rot = Rot([(alloc(256, F32), Tl()) for _ in range(3)])
        t2_rot = Rot([(alloc(256, F32), Tl()) for _ in range(2)])
        yn_rot = Rot([(alloc(256, BF16), Tl()) for _ in range(2)])
        ssq = alloc(2, F32); ssq_t = Tl()
        t12 = alloc(512, F32); t12_t = Tl()
        SEG_rot = Rot([((kb.bank(0), Tl(psum=True)), (kb.bank(1), Tl(psum=True))), ((kb.bank(2), Tl(psum=True)), (kb.bank(3), Tl(psum=True)))])
        pZT = kb.bank(4); pZT_t = Tl(psum=True)
        pYb = kb.bank(5); pY_t = Tl(psum=True)
        pO = kb.bank(6); pO_t = Tl(psum=True)
        pst = kb.bank(7); pst_t = Tl(psum=True)
        pGb = pst; pG_t = pst_t
        pst_rot = Rot([(kb.bank(4), pZT_t), (kb.bank(5), pY_t), (kb.bank(6), pO_t), (kb.bank(7), pst_t)])
        pZ_t = pZT_t; pT_t = pZT_t
        pp_rot = Rot([(kb.bank(0), SEG_rot.tiles[0][0][1]), (kb.bank(1), SEG_rot.tiles[0][1][1])])
        pc_rot = Rot([(kb.bank(2), SEG_rot.tiles[1][0][1]), (kb.bank(3), SEG_rot.tiles[1][1][1])])
        ptr_rot = Rot([(kb.bank_bf(4), pZT_t), (kb.bank_bf(5), pY_t)])

        for g in range(8):
            kb.dma(wz, win_v[:, :, Z0 + g * 256: Z0 + (g + 1) * 256], writes=[wz_t])
            def s1chunk(ci, cc):
                w, w_t = w_rot.next()
                kb.dma(w, win_v[:, :, X0 + cc * 128: X0 + (cc + 1) * 128], writes=[w_t])
                pre, pre_t = pre_rot.next()
                dg, dg_t = dg_rot.next()
                kb.op("dve", lambda e: e.tensor_tensor(out=dg, in0=bc(identf.unsqueeze(1), [P, 5, 128]),
                                                       in1=bc(convw[:, cc, :].unsqueeze(2), [P, 5, 128]), op=ALU.mult),
                      reads=[identf_t, sc_t], writes=[dg_t])
                for t in range(4):
                    pp, pp_t = pp_rot.next()
                    kb.mm(pp_t, pp, [(w[:, c, :], uT[:, c, t * 512:(t + 1) * 512]) for c in range(8)], [w_t, uT_t[t]])
                    kb.op("act", lambda e: e.activation(out=pre[:, 2 + t * 512: 2 + (t + 1) * 512], in_=pp, func=AF.Copy),
                          reads=[pp_t], writes=[pre_t])
                if ci < 2:
                    dst, dst_t = xfm_rot.next()
                elif ci == 2:
                    dst, dst_t = Bfm, Bfm_t
                else:
                    dst, dst_t = Cfm, Cfm_t
                for t in range(4):
                    pc, pc_t = pc_rot.next()
                    kb.mm(pc_t, pc, [(dg[:, k, :], pre[:, t * 512 + k: t * 512 + k + 512]) for k in range(5)], [dg_t, pre_t])
                    kb.op("act", lambda e: e.activation(out=dst[:, t * 512:(t + 1) * 512], in_=pc, func=AF.Silu, bias=convb[:, cc:cc + 1]),
                          reads=[pc_t, sc_t], writes=[dst_t])
                if ci < 3:
                    for j0 in (0, 8):
                        ptr, ptr_t = ptr_rot.next()
                        kb.deps("pe", [dst_t, identb_t], [ptr_t])
                        inst = None
                        for jj in range(8):
                            inst = kb.tr_raw(ptr[:, jj * 128:(jj + 1) * 128], dst[:, (j0 + jj) * 128:(j0 + jj + 1) * 128], identb)
                        kb.mm_done(inst, [dst_t, identb_t], [ptr_t])
                        if ci < 2:
                            o_ap, o_t = xtm[:, j0:j0 + 8, ci * 128:(ci + 1) * 128], xtm_t
                        else:
                            o_ap, o_t = Btm[:, j0:j0 + 8, :], Btm_t
                        kb.op("dve", lambda e: e.tensor_copy(out=o_ap, in_=ptr.rearrange("p (j d) -> p j d", j=8)),
                              reads=[ptr_t], writes=[o_t])
            def s2a():
                kb.op("pool", lambda e: e.tensor_copy(out=b8.rearrange("p (d r) -> p d r", d=2),
                                                      in_=sc[:, 192:256].rearrange("p (d h) -> p d h", d=2)[:, :, 4 * g:4 * g + 4]),
                      reads=[sc_t], writes=[b8_t])
                kb.op("act", lambda e: e.activation(out=a8.rearrange("p (d r) -> p d r", d=2),
                                                    in_=sc[:, 256:320].rearrange("p (d h) -> p d h", d=2)[:, :, 4 * g:4 * g + 4], func=AF.Exp),
                      reads=[sc_t], writes=[a8_t])
                kb.op("dve", lambda e: e.tensor_scalar(out=a8, in0=a8, scalar1=-1.0, scalar2=None, op0=ALU.mult), writes=[a8_t])
                for r in range(4):
                    kb.op("dve", lambda e: e.tensor_scalar(out=identD[:, r, :], in0=identf, scalar1=sc[:, 320 + 4 * g + r: 321 + 4 * g + r],
                                                           scalar2=None, op0=ALU.mult), reads=[identf_t, sc_t], writes=[identD_t])
                pdt, pdt_t = pp_rot.next()
                kb.deps("pe", [wdt_t] + uT_t, [pdt_t])
                inst = None
                wdt_g = wdt.rearrange("p c (d h) -> p c d h", d=2)
                for j in range(16):
                    for c in range(8):
                        inst = kb.mm_raw(pdt[:, j * 8:(j + 1) * 8], uT[:, c, j * 128:(j + 1) * 128], wdt_g[:, c, :, 4 * g:4 * g + 4], c == 0, c == 7)
                kb.mm_done(inst, [wdt_t] + uT_t, [pdt_t])
                j3 = lambda ap: ap.rearrange("p (j h) -> p j h", j=16)
                b8b = bc(b8.unsqueeze(1), [P, 16, 8])
                a8b = bc(a8.unsqueeze(1), [P, 16, 8])
                kb.op("dve", lambda e: e.tensor_tensor(out=dt, in0=j3(pdt[:, 0:128]), in1=b8b, op=ALU.add), reads=[pdt_t, b8_t], writes=[dt_t])
                kb.op("act", lambda e: e.activation(out=dt, in_=dt, func=AF.Exp), writes=[dt_t])
                kb.op("act", lambda e: e.activation(out=dt, in_=dt, func=AF.Ln, bias=1.0), writes=[dt_t])
                kb.op("act", lambda e: e.activation(out=lndt, in_=dt, func=AF.Ln), reads=[dt_t], writes=[lndt_t])
                kb.op("dve", lambda e: e.tensor_tensor(out=dA, in0=dt, in1=a8b, op=ALU.mult), reads=[dt_t, a8_t], writes=[dA_t])
            def s2b():
                j3 = lambda ap: ap.rearrange("p (j h) -> p j h", j=16)
                pcs, pcs_t = pp_rot.next()
                kb.deps("pe", [dA_t, tri_t, onesf_t], [pcs_t])
                inst = None
                for j in range(16):
                    kb.mm_raw(pcs[:, j * 8:j * 8 + 4], triU, dA[:, j, 0:4], True, True)
                    kb.mm_raw(pcs[:, j * 8 + 4:j * 8 + 8], triL, dA[:, j, 4:8], True, True)
                    inst = kb.mm_raw(pcs[:, 128 + j * 8:128 + (j + 1) * 8], onesf, dA[:, j, :], True, True)
                kb.mm_done(inst, [dA_t, tri_t, onesf_t], [pcs_t])
                kb.op("act", lambda e: e.activation(out=Ac, in_=j3(pcs[:, 0:128]), func=AF.Copy), reads=[pcs_t], writes=[Ac_t])
                kb.op("act", lambda e: e.activation(out=tot, in_=j3(pcs[:, 128:256]), func=AF.Copy), reads=[pcs_t], writes=[tot_t])
                kb.op("dve", lambda e: e.tensor_tensor(out=nb, in0=lndt, in1=Ac, op=ALU.subtract), reads=[lndt_t, Ac_t], writes=[nb_t])
                kb.op("dve", lambda e: e.tensor_tensor(out=wS, in0=tot, in1=nb, op=ALU.add), reads=[tot_t, nb_t], writes=[wS_t])
                kb.op("act", lambda e: e.activation(out=wS, in_=wS, func=AF.Exp), writes=[wS_t])
                kb.op("act", lambda e: e.activation(out=eA, in_=Ac, func=AF.Exp), reads=[Ac_t], writes=[eA_t])
                kb.op("act", lambda e: e.activation(out=cd, in_=tot, func=AF.Exp), reads=[tot_t], writes=[cd_t])
            chs = (2 * g, 2 * g + 1, 16 + g, 24 + g)
            s2a()
            s1chunk(0, chs[0])
            s2b()
            for ci_ in range(1, 4):
                s1chunk(ci_, chs[ci_])
            orders = (list(range(16)), list(range(15, -1, -1)))
            for d in range(2):
                pv_, pv_t = prev[d]
                kb.op("pool", lambda e: e.memset(pv_[:, orders[d][0], :], 0.0), writes=[pv_t])
                xwa, xwa_t = xwas[d]
                kb.op("dve", lambda e: e.tensor_tensor(out=xwa.rearrange("p j (r q) -> p j r q", r=4),
                                                       in0=xtm.rearrange("p j (r q) -> p j r q", r=4),
                                                       in1=bc(wS[:, :, 4 * d:4 * d + 4].unsqueeze(3), [P, 16, 4, 64]), op=ALU.mult),
                      reads=[xtm_t, wS_t], writes=[xwa_t])
            for n_ in range(15):
                for d in range(2):
                    pv_, pv_t = prev[d]
                    H, H_t = Hs[d]
                    xwa, xwa_t = xwas[d]
                    c = orders[d][n_]
                    ps_, ps_t = pst_rot.next()
                    kb.mm(ps_t, ps_[:, 0:256], [(Btm[:, c, :], xwa[:, c, :])], [Btm_t, xwa_t])
                    if n_ == 0:
                        kb.op("dve", lambda e: e.tensor_copy(out=H, in_=ps_[:, 0:256]), reads=[ps_t], writes=[H_t])
                    else:
                        kb.op("dve", lambda e: e.tensor_tensor(out=H.rearrange("p (r q) -> p r q", r=4), in0=H.rearrange("p (r q) -> p r q", r=4),
                                                               in1=bc(cd[:, c, 4 * d:4 * d + 4].unsqueeze(2), [P, 4, 64]), op=ALU.mult),
                              reads=[cd_t], writes=[H_t])
                        kb.op("dve", lambda e: e.tensor_tensor(out=H, in0=H, in1=ps_[:, 0:256], op=ALU.add), reads=[ps_t], writes=[H_t])
                    kb.op("act", lambda e: e.activation(out=pv_[:, orders[d][n_ + 1], :], in_=H, func=AF.Copy), reads=[H_t], writes=[pv_t])
            keep = {}
            r4 = lambda ap: ap.rearrange("p (r q) -> p r q", r=4)

            keepA = {}

            def stA1(c):
                csl = slice(c * 128, (c + 1) * 128)
                kb.mm(pG_t, pGb[:, 0:128], [(Bfm[:, csl], Cfm[:, csl])], [Bfm_t, Cfm_t])
                kb.mm(pZ_t, pZT[:, 0:256], [(uT[:, cc_, csl], wz[:, cc_, :]) for cc_ in range(8)], [uT_t[c // 4], wz_t])
                Gt, Gt_t = Gt_rot.next()
                kb.op("act", lambda e: e.activation(out=Gt, in_=pGb[:, 0:128], func=AF.Copy), reads=[pG_t], writes=[Gt_t])
                sz, sz_t = sz_rot.next()
                kb.op("act", lambda e: e.activation(out=sz, in_=pZT[:, 0:256], func=AF.Exp, scale=-1.0), reads=[pZ_t], writes=[sz_t])
                segs = SEG_rot.next()
                Es = E_rot.next()
                for d in range(2):
                    sg_, sg_t = segs[d]
                    kb.deps("pe", [identb_t, mk_t, dA_t, tri_t], [sg_t])
                    kb.mm_raw(sg_, identb, mk[:, d].rearrange("p r l -> p (r l)"), True, False)
                    inst = None
                    for r in range(4):
                        inst = kb.mm_raw(sg_[:, r * 128:(r + 1) * 128], bc(dA[:, c, 4 * d + r:4 * d + r + 1], [P, 128]),
                                         triU if d == 0 else triL, False, True)
                    kb.mm_done(inst, [identb_t, mk_t, dA_t, tri_t], [sg_t])
                    E, E_t = Es[d]
                    for r in range(4):
                        kb.op("act", lambda e: e.activation(out=E[:, r * 128:(r + 1) * 128], in_=sg_[:, r * 128:(r + 1) * 128], func=AF.Exp,
                                                            bias=nb[:, c, 4 * d + r:4 * d + r + 1]),
                              reads=[sg_t, nb_t], writes=[E_t])
                kb.op("dve", lambda e: e.tensor_scalar(out=sz, in0=sz, scalar1=1.0, scalar2=None, op0=ALU.add), writes=[sz_t])
                kb.op("dve", lambda e: e.reciprocal(sz, sz), writes=[sz_t])
                kb.op("dve", lambda e: e.tensor_tensor(out=sz, in0=pZT[:, 0:256], in1=sz, op=ALU.mult), reads=[], writes=[sz_t, pZ_t])
                keepA[c] = (Es, Gt, Gt_t, sz, sz_t)

            def stA2(c):
                Es, Gt, Gt_t, sz, sz_t = keepA.pop(c)
                (Ef, Ef_t), (Eb, Eb_t) = Es
                kb.op("dve", lambda e: e.tensor_tensor(out=Ef, in0=Ef, in1=Eb, op=ALU.add), reads=[Eb_t], writes=[Ef_t])
                Mt, Mt_t = Mt_rot.next()
                kb.op("dve", lambda e: e.tensor_tensor(out=Mt, in0=Ef.rearrange("p (r l) -> p r l", r=4),
                                                       in1=bc(Gt.unsqueeze(1), [P, 4, 128]), op=ALU.mult),
                      reads=[Ef_t, Gt_t], writes=[Mt_t])
                keep[c] = (Mt, Mt_t, sz, sz_t)

            def stB(c):
                csl = slice(c * 128, (c + 1) * 128)
                Mt, Mt_t, sz, sz_t = keep[c]
                kb.deps("pe", [Mt_t, xtm_t, identD_t], [pY_t])
                inst = None
                for r in range(4):
                    kb.mm_raw(pYb[:, r * 64:(r + 1) * 64], Mt[:, r, :], xtm[:, c, r * 64:(r + 1) * 64], True, False)
                    inst = kb.mm_raw(pYb[:, r * 64:(r + 1) * 64], identD[:, r, :], xtm[:, c, r * 64:(r + 1) * 64], False, True)
                kb.mm_done(inst, [Mt_t, xtm_t, identD_t], [pY_t])
                kb.deps("pe", [Cfm_t, prev[0][1], prev[1][1]], [pO_t])
                kb.mm_raw(pO[:, 0:256], Cfm[:, csl], prev[0][0][:, c, :], True, True)
                inst = kb.mm_raw(pO[:, 256:512], Cfm[:, csl], prev[1][0][:, c, :], True, True)
                kb.mm_done(inst, [Cfm_t, prev[0][1], prev[1][1]], [pO_t])
                t2, t2_t = t2_rot.next()
                kb.op("dve", lambda e: e.tensor_tensor(out=t12.rearrange("p (a q) -> p a q", a=8), in0=pO.rearrange("p (a q) -> p a q", a=8),
                                                       in1=bc(eA[:, c, :].unsqueeze(2), [P, 8, 64]), op=ALU.mult),
                      reads=[pO_t, eA_t], writes=[t12_t])
                kb.op("dve", lambda e: e.tensor_tensor(out=t12[:, 0:256], in0=t12[:, 0:256], in1=t12[:, 256:512], op=ALU.add), writes=[t12_t])
                kb.op("dve", lambda e: e.tensor_tensor(out=t2, in0=pYb[:, 0:256], in1=t12[:, 0:256], op=ALU.add), reads=[pY_t, t12_t], writes=[t2_t])
                kb.op("dve", lambda e: e.tensor_tensor(out=t2, in0=t2, in1=sz, op=ALU.mult), reads=[sz_t], writes=[t2_t])
                keep[c] = (t2, t2_t)

            def stB2(c):
                t2, t2_t = keep[c]
                kb.op("dve", lambda e: e.memset(ssq[:, 0:1], 0.0), writes=[ssq_t])
                yn, yn_t = yn_rot.next()
                kb.op("act", lambda e: e.activation(out=yn, in_=t2, func=AF.Square, accum_out=ssq[:, 0:1]), reads=[t2_t], writes=[yn_t, ssq_t])
                kb.op("act", lambda e: e.activation(out=ssq[:, 0:1], in_=ssq[:, 0:1], func=AF.Ln, bias=EPS, scale=1.0 / 256), writes=[ssq_t])
                kb.op("act", lambda e: e.activation(out=ssq[:, 0:1], in_=ssq[:, 0:1], func=AF.Exp, scale=-0.5), writes=[ssq_t])
                kb.op("act", lambda e: e.activation(out=yn, in_=t2, func=AF.Copy, scale=ssq[:, 0:1]),
                      reads=[t2_t, ssq_t], writes=[yn_t])
                keep[c] = (yn, yn_t)

            def stC(c):
                csl = slice(c * 128, (c + 1) * 128)
                yn, yn_t = keep.pop(c)
                pTb = pZT[:, 256:384].bitcast(BF16)
                kb.deps("pe", [yn_t, identb_t], [pT_t])
                kb.tr_raw(pTb[:, 0:128], yn[:, 0:128], identb)
                inst = kb.tr_raw(pTb[:, 128:256], yn[:, 128:256], identb)
                kb.mm_done(inst, [yn_t, identb_t], [pT_t])
                oc = (2 * g) % 4
                kb.op("dve", lambda e: e.tensor_tensor(out=oT[:, oc:oc + 2, csl], in0=pTb.rearrange("p (k t) -> p k t", k=2),
                                                       in1=bc(gains[:, G_SSDN + 2 * g: G_SSDN + 2 * g + 2].unsqueeze(2), [P, 2, 128]), op=ALU.mult),
                      reads=[pT_t, gains_t], writes=[oT_t])
            for i in range(21):
                if 0 <= i - 1 < 16:
                    stA2(i - 1)
                if i < 16:
                    stA1(i)
                if 0 <= i - 2 < 16:
                    stB(i - 2)
                if 0 <= i - 3 < 16:
                    stB2(i - 3)
                if 0 <= i - 4 < 16:
                    stC(i - 4)
            if g % 2 == 1:
                L["mp"] = Rot([(kb.bank(0), SEG_rot.tiles[0][0][1]), (kb.bank(1), SEG_rot.tiles[0][1][1]),
                               (kb.bank(2), SEG_rot.tiles[1][0][1]), (kb.bank(3), SEG_rot.tiles[1][1][1])])
                merge_branch("ssd", "w_br_ssd", (g - 1) * 256, oT, oT_t, first, L, nk=4, gate=("store" if g == 1 else "load"))
                first = False

    def phase_B(s):
        first = True
        if "ssd" in BR:
            ssd_branch(s, first)
            first = False
            kb.barrier()
        if "na" in BR:
            na_branch(s, first)
            first = False
            kb.barrier()
        if "mem" in BR:
            mem_branch(s, first)
            first = False
            kb.barrier()
        if first:
            for t in range(4):
                kb.op("pool", lambda e: e.memset(mg[:, :, t * 512:(t + 1) * 512], 0.0), writes=[mg_t[t]])

    nseq = NSEQ if debug is None else debug.get("nseq", NSEQ)
    for n_ in FAST:
        need(n_)
    for e_ in ("dve", "act", "pe"):
        for (_, t_) in stb_rot.tiles + stg_rot.tiles:
            kb.deps(e_, [], [t_])
    for s in range(nseq):
        phase_A(s)
        kb.barrier()
        phase_B(s)
        kb.barrier()
        phase_C(s)
        kb.barrier()
    kb.finish()
    return nc, kb


def _na_bias_layout(rpb):
    def rs(r):
        return min(max(r - 4, 0), 24)
    qrl = np.arange(128) // 64
    qc = np.arange(128) % 64
    krl = np.arange(128) // 64
    kc = np.arange(128) % 64
    cs = np.clip(qc - 8, 0, 48)
    col_ok = (kc[:, None] >= cs[None, :]) & (kc[:, None] < cs[None, :] + 16)
    rel_col = np.clip(kc[:, None] - qc[None, :] + 15, 0, 30)
    out = np.full((16, 128, NA_NSLOT, 128), NEG, np.float32)

    def fill(slot, j, i):
        r = 2 * i + qrl
        rsr = np.array([rs(int(v)) for v in r])
        kr = 2 * j + krl
        row_ok = (kr[:, None] >= rsr[None, :]) & (kr[:, None] < rsr[None, :] + 8)
        rel_row = np.clip(kr[:, None] - r[None, :] + 7, 0, 14)
        ok = row_ok & col_ok
        vals = rpb[:, rel_row, rel_col]
        out[:, :, slot, :] = np.where(ok[None], vals, np.float32(NEG))

    for j in (0, 1, 2, 3, 12, 13, 14, 15):
        ilo, ihi = _na_blocks(j)
        for n, i in enumerate(range(ilo, ihi + 1)):
            fill(_NA_SLOT[j] + n, j, i)
    for n, i in enumerate(range(4, 9)):
        fill(_NA_INT + n, 6, i)
    return np.ascontiguousarray(out.reshape(16 * 128, NA_NSLOT * 128))


def _host_inputs(inputs):
    xs = np.concatenate([inputs["x_prompt"], inputs["x_sample"]], axis=0)
    ms = np.concatenate([inputs["mem_prompt"], inputs["mem_sample"]], axis=0)
    common = {}
    for n in ("ffn1_w_gate", "ffn1_w_up", "ffn1_w_down", "w_in", "w_mem_kv", "w_br_na", "w_br_ssd", "w_br_mem",
              "w_out", "ffn2_w_gate", "ffn2_w_up", "ffn2_w_down"):
        common[n] = np.ascontiguousarray(inputs[n][0], dtype=np.float32)
    g = np.zeros((128, 64), np.float32)

    def col8(v):
        return np.asarray(v, np.float32).reshape(-1, 128).T

    g[:, 0:8] = col8(inputs["ffn1_norm"][0])
    g[:, 8:16] = col8(inputs["mix_norm"][0])
    g[:, 16:24] = col8(inputs["ffn2_norm"][0])
    g[:, 24:32] = col8(inputs["mem_norm"][0])
    g[:, 32] = np.tile(np.asarray(inputs["na_q_norm"][0], np.float32), 2)
    g[:, 33] = np.tile(np.asarray(inputs["na_k_norm"][0], np.float32), 2)
    g[:, 34:36] = col8(inputs["mem_q_norm"][0])
    g[:, 36:38] = col8(inputs["mem_k_norm"][0])
    g[:, 38:54] = col8(inputs["ssd_norm"][0])
    common["nab"] = _na_bias_layout(np.asarray(inputs["na_rpb"][0], np.float32))
    sc = np.zeros((128, 352), np.float32)
    cw = np.asarray(inputs["conv_w"][0], np.float32)
    sc[:, 0:160] = cw.reshape(5, 32, 128).transpose(2, 1, 0).reshape(128, 160)
    sc[:, 160:192] = np.asarray(inputs["conv_b"][0], np.float32).reshape(32, 128).T
    sc[:, 192:224] = np.asarray(inputs["dt_bias_f"][0], np.float32)[None, :]
    sc[:, 224:256] = np.asarray(inputs["dt_bias_b"][0], np.float32)[None, :]
    sc[:, 256:288] = np.asarray(inputs["a_log_f"][0], np.float32)[None, :]
    sc[:, 288:320] = np.asarray(inputs["a_log_b"][0], np.float32)[None, :]
    sc[:, 320:352] = np.asarray(inputs["ssd_d"][0], np.float32)[None, :]
    common["ssdc"] = sc
    ii = np.arange(128)
    tri = np.zeros((128, 512), np.float32)
    tri[:, 0:128] = (ii[:, None] <= ii[None, :])
    tri[:, 128:256] = (ii[:, None] >= ii[None, :])
    tri[:, 256:384] = np.where(ii[:, None] <= ii[None, :], 0.0, NEG)
    tri[:, 384:512] = np.where(ii[:, None] >= ii[None, :], 0.0, NEG)
    common["tri"] = tri
    common["gains"] = g
    common["ident"] = np.eye(128, dtype=np.float32)
    maps = []
    for i in range(8):
        m = dict(common)
        m["x"] = np.ascontiguousarray(xs[3 * i:3 * i + 3], dtype=np.float32)
        m["mem"] = np.ascontiguousarray(ms[3 * i:3 * i + 3], dtype=np.float32)
        maps.append(m)
    return maps


_CACHE = {}


def kernel(**inputs):
    if "nc" not in _CACHE:
        _CACHE["nc"] = build_program()[0]
    nc = _CACHE["nc"]
    maps = _host_inputs(inputs)
    res = run_bass_kernel_spmd(nc, maps, core_ids=list(range(8)))
    ys = np.concatenate([np.asarray(r["y"], dtype=np.float32) for r in res.results], axis=0)
    return (np.ascontiguousarray(ys[0:8]), np.ascontiguousarray(ys[8:24]))
```
